# Optimizing a Trainium2 kernel written in Bass

```python
import math, functools
import jax, jax.numpy as jnp
from jax import lax
import numpy as np

D_MODEL = 1024
BATCH = 2
SEQ = 8192
DEPTH = 1
DEC_BATCH = 128
DEC_SEQ = 1
PAST_LEN = 8192
PAGE_SIZE = 128

N_HEADS_A = 8
N_KV_A = 2
HEAD_DIM_A = 64
GROUP_A = N_HEADS_A // N_KV_A
IDX_HEADS = 8
IDX_DIM = 64
TOPK_MAX = 256
Q_BLOCK = 128
ROPE_THETA = 500000.0
H_B = 4
DK_B = 128
DV_B = 128
CONV_K = 4
CHUNK = 64
CONV_DIM = 2 * H_B * DK_B + H_B * DV_B
BRANCH_WIDTH = N_HEADS_A * HEAD_DIM_A
D_FF = -(-8 * D_MODEL // (3 * 256)) * 256
NORM_EPS = 1e-6
NEG_INF = -1e30
IN_SIZES = (N_HEADS_A * HEAD_DIM_A, N_KV_A * HEAD_DIM_A, N_KV_A * HEAD_DIM_A,
            IDX_HEADS * IDX_DIM, IDX_DIM, IDX_HEADS,
            CONV_DIM, H_B, H_B, H_B * DV_B, 2 * D_MODEL)
D_IN = sum(IN_SIZES)

kernel_name = 'hybrid_dsa_gated_delta_gated_merge_step'


def rms_norm(x, gain):
    xf = x.astype(jnp.float32)
    y = xf * lax.rsqrt(jnp.mean(xf * xf, axis=-1, keepdims=True) + NORM_EPS)
    return (y * gain.astype(jnp.float32)).astype(x.dtype)


def l2_norm(x):
    xf = x.astype(jnp.float32)
    return xf * lax.rsqrt(jnp.sum(xf * xf, axis=-1, keepdims=True) + NORM_EPS)


def rope(x, pos):
    rot = x.shape[-1] // 4
    half = rot // 2
    inv_freq = ROPE_THETA ** (-jnp.arange(half, dtype=jnp.float32) / half)
    ang = pos.astype(jnp.float32)[:, None] * inv_freq[None, :]
    cos = jnp.cos(ang)[None, :, None, :]
    sin = jnp.sin(ang)[None, :, None, :]
    xf = x.astype(jnp.float32)
    x1, x2, rest = xf[..., :half], xf[..., half:rot], xf[..., rot:]
    out = jnp.concatenate([x1 * cos - x2 * sin, x2 * cos + x1 * sin, rest], axis=-1)
    return out.astype(x.dtype)


def take_rows(a, idx):
    return jax.vmap(lambda ab, ib: ab[ib])(a, idx)


def select_keys(iq, ik, iw, qpos, topk):
    dots = jnp.einsum('bqhd,bsd->bqhs', iq.astype(jnp.float32), ik.astype(jnp.float32))
    score = jnp.einsum('bqhs,bqh->bqs', jax.nn.relu(dots * IDX_DIM ** -0.5),
                       iw.astype(jnp.float32)) * IDX_HEADS ** -0.5
    kpos = jnp.arange(ik.shape[1])
    admissible = kpos[None, None, :] <= qpos[None, :, None]
    score = jnp.where(admissible, score, NEG_INF)
    _, sel = lax.top_k(score, topk)
    valid = sel <= qpos[None, :, None]
    return sel, valid


def sparse_attend(q, ksel, vsel, valid):
    B, Q = q.shape[:2]
    qg = q.astype(jnp.float32).reshape(B, Q, N_KV_A, GROUP_A, HEAD_DIM_A)
    logits = jnp.einsum('bqngd,bqknd->bqngk', qg, ksel.astype(jnp.float32)) * HEAD_DIM_A ** -0.5
    logits = jnp.where(valid[:, :, None, None, :], logits, NEG_INF)
    p = jax.nn.softmax(logits, axis=-1)
    o = jnp.einsum('bqngk,bqknd->bqngd', p, vsel.astype(jnp.float32))
    return o.reshape(B, Q, N_HEADS_A, HEAD_DIM_A).astype(q.dtype)


def dsa_prompt(q, k, v, iq, ik, iw):
    B, T = q.shape[:2]
    topk = min(TOPK_MAX, T // 4)

    def one_block(i):
        t0 = i * Q_BLOCK

        def cut(a):
            return lax.dynamic_slice_in_dim(a, t0, Q_BLOCK, axis=1)

        qpos = t0 + jnp.arange(Q_BLOCK)
        sel, valid = select_keys(cut(iq), ik, cut(iw), qpos, topk)
        return sparse_attend(cut(q), take_rows(k, sel), take_rows(v, sel), valid)

    o = lax.map(one_block, jnp.arange(T // Q_BLOCK))
    return jnp.moveaxis(o, 0, 1).reshape(B, T, N_HEADS_A, HEAD_DIM_A)


def dsa_sample(q, k, v, iq, ik, iw, cache_k, cache_v, cache_ik, page_table):
    DB, DS = q.shape[:2]
    past = page_table.shape[1] * PAGE_SIZE
    topk = min(TOPK_MAX, (past + DS) // 4)
    ik_past = cache_ik[page_table].reshape(DB, past, IDX_DIM)
    ik_all = jnp.concatenate([ik_past, ik.astype(ik_past.dtype)], axis=1)
    qpos = past + jnp.arange(DS)
    sel, valid = select_keys(iq, ik_all, iw, qpos, topk)
    in_past = (sel < past)[..., None, None]
    p = jnp.minimum(sel, past - 1)
    phys = jax.vmap(lambda pt, lp: pt[lp])(page_table, p // PAGE_SIZE)
    slot = p % PAGE_SIZE
    j = jnp.clip(sel - past, 0, DS - 1)
    ksel = jnp.where(in_past, cache_k[phys, slot], take_rows(k, j).astype(cache_k.dtype))
    vsel = jnp.where(in_past, cache_v[phys, slot], take_rows(v, j).astype(cache_v.dtype))
    return sparse_attend(q, ksel, vsel, valid)


def gated_delta_chunked(q, k, v, g, beta, s0):
    B, T, H, DK = q.shape
    DV = v.shape[-1]
    n = -(-T // CHUNK)
    pad = n * CHUNK - T

    def prep(a):
        a = a.astype(jnp.float32)
        a = jnp.pad(a, [(0, 0), (0, pad)] + [(0, 0)] * (a.ndim - 2))
        a = a.reshape((B, n, CHUNK) + a.shape[2:])
        return jnp.moveaxis(a, 3, 1)

    q, k, v, g, beta = prep(q), prep(k), prep(v), prep(g), prep(beta)
    g = jnp.cumsum(g, axis=-1)
    idx = jnp.arange(CHUNK)
    lower = idx[:, None] >= idx[None, :]
    strict = idx[:, None] > idx[None, :]
    diff = g[..., :, None] - g[..., None, :]
    decay = jnp.where(lower, jnp.exp(jnp.where(lower, diff, 0.0)), 0.0)
    kb = k * beta[..., None]
    vb = v * beta[..., None]
    a_mat = jnp.where(strict, jnp.einsum('bhncd,bhnsd->bhncs', kb, k) * decay, 0.0)
    eye = jnp.eye(CHUNK, dtype=jnp.float32)
    rhs = jnp.concatenate([vb, kb * jnp.exp(g)[..., None]], axis=-1)
    sol = lax.linalg.triangular_solve(eye + a_mat, rhs, left_side=True, lower=True)
    u, w = sol[..., :DV], sol[..., DV:]
    qk = jnp.einsum('bhncd,bhnsd->bhncs', q, k) * decay
    q_dec = q * jnp.exp(g)[..., None]
    k_dec = k * jnp.exp(g[..., -1:] - g)[..., None]
    g_last = jnp.exp(g[..., -1])

    def step(s, xs):
        u_i, w_i, qd_i, qk_i, kd_i, gl_i = xs
        v_new = u_i - jnp.einsum('bhcd,bhde->bhce', w_i, s)
        o = jnp.einsum('bhcd,bhde->bhce', qd_i, s) + jnp.einsum('bhcs,bhse->bhce', qk_i, v_new)
        s = s * gl_i[..., None, None] + jnp.einsum('bhcd,bhce->bhde', kd_i, v_new)
        return s, o

    xs = tuple(jnp.moveaxis(a, 2, 0) for a in (u, w, q_dec, qk, k_dec, g_last))
    s_fin, o = lax.scan(step, s0.astype(jnp.float32), xs)
    o = jnp.transpose(o, (1, 0, 3, 2, 4)).reshape(B, n * CHUNK, H, DV)[:, :T]
    return o, s_fin


def block(x, positions, attend, conv_buf, s0, norm_mix, w_in, q_norm, k_norm, w_conv, a_log,
          dt_bias, delta_norm, w_branch, w_out, norm_ffn, w_gate_up, w_down):
    B, T, _ = x.shape
    xn = rms_norm(x, norm_mix)
    h = xn @ w_in
    pts = np.cumsum(IN_SIZES)[:-1].tolist()
    q, k, v, iq, ik, iw, u, a, b, z, gl = jnp.split(h, pts, axis=-1)
    q = rope(rms_norm(q.reshape(B, T, N_HEADS_A, HEAD_DIM_A), q_norm), positions)
    k = rope(rms_norm(k.reshape(B, T, N_KV_A, HEAD_DIM_A), k_norm), positions)
    v = v.reshape(B, T, N_KV_A, HEAD_DIM_A)
    iq = rope(iq.reshape(B, T, IDX_HEADS, IDX_DIM), positions)
    ik = rope(ik.reshape(B, T, 1, IDX_DIM), positions).reshape(B, T, IDX_DIM)
    o_a = attend(q, k, v, iq, ik, iw).reshape(B, T, BRANCH_WIDTH)
    u_cat = jnp.concatenate([conv_buf.astype(u.dtype), u], axis=1)
    c = jax.nn.silu(sum(u_cat[:, j:j + T] * w_conv[j] for j in range(CONV_K)))
    new_conv = u_cat[:, T:]
    qb, kb, vb = jnp.split(c, [H_B * DK_B, 2 * H_B * DK_B], axis=-1)
    qb = l2_norm(qb.reshape(B, T, H_B, DK_B)) * DK_B ** -0.5
    kb = l2_norm(kb.reshape(B, T, H_B, DK_B))
    vb = vb.reshape(B, T, H_B, DV_B)
    g = -jnp.exp(a_log.astype(jnp.float32)) * jax.nn.softplus(a.astype(jnp.float32) + dt_bias.astype(jnp.float32))
    beta = jax.nn.sigmoid(b.astype(jnp.float32))
    o_b, s_new = gated_delta_chunked(qb, kb, vb, g, beta, s0)
    o_b = rms_norm(o_b.astype(x.dtype), delta_norm) * jax.nn.silu(z.reshape(B, T, H_B, DV_B))
    o_b = o_b.reshape(B, T, BRANCH_WIDTH)
    proj = jnp.einsum('btnc,ncd->btnd', jnp.stack([o_a, o_b], axis=2), w_branch)
    gates = jax.nn.sigmoid(gl.reshape(B, T, 2, D_MODEL))
    x = x + jnp.sum(gates * proj, axis=2) @ w_out
    hn = rms_norm(x, norm_ffn)
    gg, uu = jnp.split(hn @ w_gate_up, 2, axis=-1)
    x = x + (jax.nn.silu(gg) * uu) @ w_down
    return x, (k, v, ik, new_conv, s_new.astype(s0.dtype))


def setup_inputs(seed: int = 0) -> dict:
    key = jax.random.key(seed)
    ks = jax.random.split(key, 24)
    n_pages = PAST_LEN // PAGE_SIZE
    n_used = DEC_BATCH * n_pages
    n_pool = n_used + (n_used + 3) // 4

    def nrm(k, shape, scale=1.0):
        return scale * jax.random.normal(k, shape, jnp.float32)

    page_table = jax.random.permutation(ks[0], n_pool)[:n_used].reshape(DEC_BATCH, n_pages).astype(jnp.int32)
    dt = jnp.exp(jax.random.uniform(ks[1], (DEPTH, H_B), jnp.float32, math.log(1e-3), math.log(1e-1)))
    return {
        'x_prompt': nrm(ks[2], (BATCH, SEQ, D_MODEL)),
        'x_sample': nrm(ks[3], (DEC_BATCH, DEC_SEQ, D_MODEL)),
        'cache_k': nrm(ks[4], (DEPTH, n_pool, PAGE_SIZE, N_KV_A, HEAD_DIM_A)),
        'cache_v': nrm(ks[5], (DEPTH, n_pool, PAGE_SIZE, N_KV_A, HEAD_DIM_A)),
        'cache_idx_k': nrm(ks[6], (DEPTH, n_pool, PAGE_SIZE, IDX_DIM)),
        'state_conv': nrm(ks[7], (DEPTH, DEC_BATCH, CONV_K - 1, CONV_DIM)),
        'state_delta': nrm(ks[8], (DEPTH, DEC_BATCH, H_B, DK_B, DV_B), DK_B ** -0.5),
        'page_table': page_table,
        'norm_mix': 1.0 + nrm(ks[9], (DEPTH, D_MODEL), 0.02),
        'w_in': nrm(ks[10], (DEPTH, D_MODEL, D_IN), D_MODEL ** -0.5),
        'q_norm': 1.0 + nrm(ks[11], (DEPTH, HEAD_DIM_A), 0.02),
        'k_norm': 1.0 + nrm(ks[12], (DEPTH, HEAD_DIM_A), 0.02),
        'w_conv': nrm(ks[13], (DEPTH, CONV_K, CONV_DIM), CONV_K ** -0.5),
        'a_log': jnp.log(jax.random.uniform(ks[14], (DEPTH, H_B), jnp.float32, 1.0, 16.0)),
        'dt_bias': dt + jnp.log(-jnp.expm1(-dt)),
        'delta_norm': 1.0 + nrm(ks[15], (DEPTH, DV_B), 0.02),
        'w_branch': nrm(ks[16], (DEPTH, 2, BRANCH_WIDTH, D_MODEL), BRANCH_WIDTH ** -0.5),
        'w_out': nrm(ks[17], (DEPTH, D_MODEL, D_MODEL), D_MODEL ** -0.5),
        'norm_ffn': 1.0 + nrm(ks[18], (DEPTH, D_MODEL), 0.02),
        'w_gate_up': nrm(ks[19], (DEPTH, D_MODEL, 2 * D_FF), D_MODEL ** -0.5),
        'w_down': nrm(ks[20], (DEPTH, D_FF, D_MODEL), D_FF ** -0.5),
    }


def reference(x_prompt, x_sample, cache_k, cache_v, cache_idx_k, state_conv, state_delta, page_table,
              norm_mix, w_in, q_norm, k_norm, w_conv, a_log, dt_bias, delta_norm, w_branch, w_out,
              norm_ffn, w_gate_up, w_down):
    B, T, _ = x_prompt.shape
    DS = x_sample.shape[1]
    past = page_table.shape[1] * PAGE_SIZE
    pos_p = jnp.arange(T)
    pos_s = past + jnp.arange(DS)
    conv0 = jnp.zeros((B, CONV_K - 1, CONV_DIM), x_prompt.dtype)
    delta0 = jnp.zeros((B, H_B, DK_B, DV_B), x_prompt.dtype)
    y_prompt, y_sample = x_prompt, x_sample
    new_p, new_s = [], []
    for l in range(DEPTH):
        weights = (norm_mix[l], w_in[l], q_norm[l], k_norm[l], w_conv[l], a_log[l], dt_bias[l],
                   delta_norm[l], w_branch[l], w_out[l], norm_ffn[l], w_gate_up[l], w_down[l])
        y_prompt, st_p = block(y_prompt, pos_p, dsa_prompt, conv0, delta0, *weights)
        attend_s = functools.partial(dsa_sample, cache_k=cache_k[l], cache_v=cache_v[l],
                                     cache_ik=cache_idx_k[l], page_table=page_table)
        y_sample, st_s = block(y_sample, pos_s, attend_s, state_conv[l], state_delta[l], *weights)
        new_p.append(st_p)
        new_s.append(st_s)
    k_p, v_p, ik_p, conv_p, delta_p = [jnp.stack(a) for a in zip(*new_p)]
    k_s, v_s, ik_s, conv_s, delta_s = [jnp.stack(a) for a in zip(*new_s)]
    return (y_prompt, y_sample, k_p, v_p, ik_p, conv_p, delta_p, k_s, v_s, ik_s, conv_s, delta_s)
```

```python
from contextlib import ExitStack
import os
import numpy as np
import concourse.bass as bass
import concourse.mybir as mybir
from concourse.bass_utils import run_bass_kernel_spmd

F32 = mybir.dt.float32
BF16 = mybir.dt.bfloat16
I32 = mybir.dt.int32
AF = mybir.ActivationFunctionType
ALU = mybir.AluOpType
AX = mybir.AxisListType

D = 1024
T = 8192
NT = 64
NOWN = 16
NS = 16
DFF = 2816
EPS = 1e-6
NDS = 24
TOPK = 256
NITER = 30
NPOOL = 10240


ALL_TL = []


class Tl:
    def __init__(self, t):
        self.t = t
        self.w = None
        self.r = {}
        ALL_TL.append(self)

    def __getitem__(self, idx):
        return self.t[idx]


class Sch:
    def __init__(self, nc, es):
        self.nc = nc
        self.eng = {}
        for name, e in (('pe', nc.tensor), ('act', nc.scalar), ('dve', nc.vector),
                        ('pool', nc.gpsimd), ('sp', nc.sync)):
            sem = es.enter_context(nc.semaphore('s_' + name))
            self.eng[name] = dict(e=e, sem=sem, cnt=0, known={}, name=name)
        self.dsem = [es.enter_context(nc.semaphore('d%d' % i)) for i in range(NDS)]
        self.dtot = [0] * NDS
        self.dnext = 0
        self.sems = {}
        self.bsem = es.enter_context(nc.semaphore('bar_b'))
        self.rsem = es.enter_context(nc.semaphore('bar_r'))
        self.epoch = 0
        self.dq = 0

    def _deps(self, reads, writes, E):
        deps = {}

        def add(st):
            sem, val = st
            k = id(sem)
            self.sems[k] = sem
            if deps.get(k, 0) < val:
                deps[k] = val
        for t in reads:
            if t.w is not None:
                add(t.w)
        for t in writes:
            if t.w is not None:
                add(t.w)
            for k, v in t.r.items():
                add((self.sems[k], v))
        for k, val in deps.items():
            if E['name'] == 'pe' and k == id(E['sem']):
                continue
            if E['known'].get(k, 0) < val:
                E['e'].wait_ge(self.sems[k], val)
                E['known'][k] = val

    def _stamp(self, reads, writes, st):
        sem, val = st
        k = id(sem)
        self.sems[k] = sem
        for t in reads:
            if t.r.get(k, 0) < val:
                t.r[k] = val
        for t in writes:
            t.w = st
            t.r = {}

    def op(self, en, fn, reads, writes):
        E = self.eng[en]
        self._deps(reads, writes, E)
        ins = fn(E['e'])
        E['cnt'] += 1
        ins.then_inc(E['sem'], 1)
        self._stamp(reads, writes, (E['sem'], E['cnt']))

    def dma(self, fn, reads, writes, en=None):
        if en is None:
            en = ('sp', 'act')[self.dq % 2]
            self.dq += 1
        E = self.eng[en]
        self._deps(reads, writes, E)
        k = self.dnext
        self.dnext = (self.dnext + 1) % NDS
        sem = self.dsem[k]
        self.sems[id(sem)] = sem
        if E['known'].get(id(sem), 0) < self.dtot[k]:
            E['e'].wait_ge(sem, self.dtot[k])
            E['known'][id(sem)] = self.dtot[k]
        ins = fn(E['e'])
        ins.then_inc(sem, 16)
        self.dtot[k] += 16
        self._stamp(reads, writes, (sem, self.dtot[k]))

    def barrier(self):
        for en, E in self.eng.items():
            for en2, E2 in self.eng.items():
                if en2 == en:
                    continue
                k = id(E2['sem'])
                self.sems[k] = E2['sem']
                if E2['cnt'] > 0 and E['known'].get(k, 0) < E2['cnt']:
                    E['e'].wait_ge(E2['sem'], E2['cnt'])
                    E['known'][k] = E2['cnt']
            for i in range(NDS):
                k = id(self.dsem[i])
                if self.dtot[i] > 0 and E['known'].get(k, 0) < self.dtot[i]:
                    E['e'].wait_ge(self.dsem[i], self.dtot[i])
                    E['known'][k] = self.dtot[i]


def _sch_reset(self):
    self.barrier()
    self.epoch += 1
    for en, E in self.eng.items():
        E['e'].sem_inc(self.bsem, 1)
    pool = self.eng['pool']['e']
    pool.wait_ge(self.bsem, 5 * self.epoch)
    for en, E in self.eng.items():
        pool.sem_clear(E['sem'])
    for d in self.dsem:
        pool.sem_clear(d)
    pool.sem_inc(self.rsem, 1)
    for en, E in self.eng.items():
        E['e'].wait_ge(self.rsem, self.epoch)
        E['cnt'] = 0
        E['known'] = {}
    self.dtot = [0] * NDS
    for t in ALL_TL:
        t.w = None
        t.r = {}


Sch.reset = _sch_reset


def build(phase_limit=99):
    nc = bass.Bass("TRN2", target_bir_lowering=False)

    def din(name, shape, dt=F32):
        return nc.dram_tensor(name, list(shape), dt, kind="ExternalInput").ap()

    def dout(name, shape, dt=F32):
        return nc.dram_tensor(name, list(shape), dt, kind="ExternalOutput").ap()

    def dscr(name, shape, dt=F32):
        return nc.dram_tensor(name, list(shape), dt, kind="Internal").ap()

    xseq = din("xseq", [T, D])
    xown = din("xown", [17 * 128, D])
    rope_seq = din("rope_seq", [T, 16])
    rope_own = din("rope_own", [17 * 128, 16])
    sel_in = din("sel", [128, 4])
    adm_in = din("adm", [128, 1024])
    cst_in = din("cst", [128, 6 * 128])
    oh_in = din("oh", [128, 256])
    iota_in = din("iota", [128, 1])
    w_seq = din("w_seq", [D, 1864])
    w_q = din("w_q", [D, 1032])
    w_gz = din("w_gz", [D, 2560])
    nmix = din("nmix", [128, 8])
    nffn = din("nffn", [128, 8])
    qn_g = din("qn_g", [1, 64])
    kn_g = din("kn_g", [1, 64])
    wconv = din("wconv", [4, 1536])
    alog = din("alog", [1, 4])
    dtb = din("dtb", [1, 4])
    dn_g = din("dn_g", [1, 128])
    w_br = din("w_br", [2, 512, D])
    w_out = din("w_out", [D, D])
    w_gu = din("w_gu", [D, 2 * DFF])
    w_dn = din("w_dn", [DFF, D])
    if phase_limit >= 4:
        cache_k = din("cache_k", [NPOOL * 2, 64 * 128])
        cache_v = din("cache_v", [NPOOL * 2, 64 * 128])
        cache_ik = din("cache_ik", [NPOOL * 2, 64 * 64])
    ptab2_in = din("ptab2", [128, NS], I32)
    half_in = din("halfc", [128, 1])
    st_conv = din("st_conv", [NS, 3, 1536])
    st_delta = din("st_delta", [NS, 4, 128, 128])

    y_own = dout("y_own", [17 * 128, D])
    kp = dout("kp", [T, 128])
    vp = dout("vp", [T, 128])
    ikp = dout("ikp", [T, 64])
    convp = dout("convp", [3, 1536])
    deltap = dout("deltap", [4, 128, 128])
    ks_o = dout("ks", [NS, 128])
    vs_o = dout("vs", [NS, 128])
    iks_o = dout("iks", [NS, 64])
    convs = dout("convs", [NS, 3, 1536])
    deltas = dout("deltas", [NS, 4, 128, 128])

    oown_d = dscr("oown_d", [17, 128, 512])
    oat_d = dscr("oat_d", [17, 64, 8 * 128], BF16)
    x1_d = dscr("x1_d", [17 * 128, D])

    es = ExitStack()
    S = Sch(nc, es)

    def sb(stack, name, shape, dt=F32):
        return Tl(stack.enter_context(nc.sbuf_tensor("sb_" + name, list(shape), dt)))

    def ps(stack, name, shape, dt=F32):
        return Tl(stack.enter_context(nc.psum_tensor("ps_" + name, list(shape), dt)))

    def V(fn, r, w):
        S.op('dve', fn, r, w)

    def A(fn, r, w):
        S.op('act', fn, r, w)

    def G(fn, r, w):
        S.op('pool', fn, r, w)

    def P(fn, r, w):
        S.op('pe', fn, r, w)

    def DM(out, in_, r, w):
        S.dma(lambda e: e.dma_start(out=out, in_=in_), r, w)

    cst = sb(es, "cst", [128, 6, 128])
    oh = sb(es, "oh", [128, 16, 16])
    iota = sb(es, "iota", [128, 1])
    selt = sb(es, "selt", [128, 4])
    identb = sb(es, "identb", [128, 128], BF16)
    DM(cst[:].rearrange("p a b -> p (a b)"), cst_in[:, :], [], [cst])
    DM(oh[:].rearrange("p a b -> p (a b)"), oh_in[:, :], [], [oh])
    DM(iota[:], iota_in[:, :], [], [iota])
    DM(selt[:], sel_in[:, :], [], [selt])
    ident = cst[:, 0, :]
    ones = cst[:, 1, :]
    Ustr = cst[:, 2, :]
    Uinc = cst[:, 3, :]
    Lstr = cst[:, 4, :]
    TriC = cst[:, 5, :]
    V(lambda e: e.tensor_copy(out=identb[:], in_=ident), [cst], [identb])

    qg_bc = sb(es, "qg_bc", [128, 64])
    kg_bc = sb(es, "kg_bc", [128, 64])
    ropeo = sb(es, "ropeo", [128, 17, 16])
    DM(ropeo[:], rope_own.rearrange("(n p) c -> p n c", p=128), [], [ropeo])
    if phase_limit >= 2:
        qsS = sb(es, "qsS", [128, 1032])
        QTs = sb(es, "QTs", [128, 4, 128], BF16)
        IQTs = sb(es, "IQTs", [64, 8, 128], BF16)
        wscS = sb(es, "wscS", [128, 8])
        kviS = sb(es, "kviS", [128, 328])
    sA = ExitStack()
    KT = sb(sA, "KT", [128, T], BF16)
    IKT = sb(sA, "IKT", [64, T], BF16)
    VA = sb(sA, "VA", [128, NT, 130], BF16)
    DM(qg_bc[:], qn_g[0:1, :].partition_broadcast(128), [], [qg_bc])
    DM(kg_bc[:], kn_g[0:1, :].partition_broadcast(128), [], [kg_bc])
    G(lambda e: e.memset(VA[:, :, 64:65], 1.0), [], [VA])
    G(lambda e: e.memset(VA[:, :, 129:130], 1.0), [], [VA])

    def load_weight(dst, kcn, src, ncols, stage, gain=None, col0=0, chunk=2048):
        for kc in range(kcn):
            for c0 in range(0, ncols, chunk):
                cw = min(chunk, ncols - c0)
                DM(stage[:, 0:cw], src[kc * 128:(kc + 1) * 128, col0 + c0:col0 + c0 + cw], [], [stage])
                if gain is not None:
                    V(lambda e, kc=kc, c0=c0, cw=cw: e.tensor_scalar(
                        out=dst[:, kc, c0:c0 + cw], in0=stage[:, 0:cw], scalar1=gain[:, kc:kc + 1],
                        scalar2=None, op0=ALU.mult), [stage, gain], [dst])
                else:
                    V(lambda e, kc=kc, c0=c0, cw=cw: e.tensor_copy(
                        out=dst[:, kc, c0:c0 + cw], in_=stage[:, 0:cw]), [stage], [dst])

    def rstd_of(ssq, out, n, tmp):
        A(lambda e: e.activation(out=tmp[:], in_=ssq[:], func=AF.Sqrt, scale=1.0 / n, bias=EPS), [ssq], [tmp])
        V(lambda e: e.reciprocal(out=out[:], in_=tmp[:]), [tmp], [out])

    def x_front(xt, xb, xT, psX, ssq, rstd, tmp1, junk):
        A(lambda e: e.activation(out=junk[:, 0:D], in_=xt[:], func=AF.Square, accum_out=ssq[:]), [xt], [junk, ssq])
        rstd_of(ssq, rstd, float(D), tmp1)
        V(lambda e: e.tensor_copy(out=xb[:], in_=xt[:]), [xt], [xb])
        for kc in range(8):
            P(lambda e, kc=kc: e.transpose(out=psX[:, kc * 128:(kc + 1) * 128], in_=xb[:, kc * 128:(kc + 1) * 128],
                                          identity=identb[:]), [xb, identb], [psX])
        A(lambda e: e.copy(out=xT[:].rearrange("p a b -> p (a b)"), in_=psX[:]), [psX], [xT])

    def proj(xT, W, c0, cw, pst):
        for kc in range(8):
            P(lambda e, kc=kc: e.matmul(pst[:, 0:cw], lhsT=xT[:, kc, :], rhs=W[:, kc, c0:c0 + cw],
                                       start=(kc == 0), stop=(kc == 7)), [xT, W], [pst])

    def rope(x3, nh, cs, t1, t2, t3):
        xa = x3[0]
        tl = x3[1]
        cosb = cs[:, None, 0:8].to_broadcast([128, nh, 8])
        sinb = cs[:, None, 8:16].to_broadcast([128, nh, 8])
        x1 = xa[:, :, 0:8]
        x2 = xa[:, :, 8:16]
        a1 = t1[:, 0:nh, :]
        a2 = t2[:, 0:nh, :]
        a3 = t3[:, 0:nh, :]
        V(lambda e: e.tensor_tensor(out=a1, in0=x1, in1=sinb, op=ALU.mult), [tl], [t1])
        V(lambda e: e.tensor_tensor(out=a2, in0=x2, in1=sinb, op=ALU.mult), [tl], [t2])
        V(lambda e: e.tensor_tensor(out=x1, in0=x1, in1=cosb, op=ALU.mult), [tl], [tl])
        V(lambda e: e.tensor_tensor(out=x2, in0=x2, in1=cosb, op=ALU.mult), [tl], [tl])
        V(lambda e: e.tensor_tensor(out=x1, in0=x1, in1=a2, op=ALU.subtract), [tl, t2], [tl])
        V(lambda e: e.tensor_tensor(out=x2, in0=x2, in1=a1, op=ALU.add), [tl, t1], [tl])

    def head_rms(xa, tl, nh, gbc, sq, ss, rs, tmp):
        V(lambda e: e.tensor_tensor(out=sq[:, 0:nh, :], in0=xa, in1=xa, op=ALU.mult), [tl], [sq])
        V(lambda e: e.tensor_reduce(out=ss[:, 0:nh], in_=sq[:, 0:nh, :], axis=AX.X, op=ALU.add), [sq], [ss])
        A(lambda e: e.activation(out=tmp[:, 0:nh], in_=ss[:, 0:nh], func=AF.Sqrt, scale=1.0 / 64, bias=EPS), [ss], [tmp])
        V(lambda e: e.reciprocal(out=rs[:, 0:nh], in_=tmp[:, 0:nh]), [tmp], [rs])
        V(lambda e: e.tensor_tensor(out=xa, in0=xa, in1=rs[:, 0:nh, None].to_broadcast([128, nh, 64]), op=ALU.mult),
          [tl, rs], [tl])
        V(lambda e: e.tensor_tensor(out=xa, in0=xa, in1=gbc[:, None, :].to_broadcast([128, nh, 64]), op=ALU.mult),
          [tl, gbc], [tl])

    p1 = ExitStack()
    Wseq = sb(p1, "Wseq", [128, 8, 1864], BF16)
    ctmp = sb(p1, "ctmp", [128, 1536])
    wstage = ctmp
    gmix = sb(p1, "gmix", [128, 8])
    DM(gmix[:], nmix[:, :], [], [gmix])
    load_weight(Wseq, 8, w_seq, 1864, wstage, gain=gmix, chunk=1536)
    wc_bc = sb(p1, "wc_bc", [128, 4, 1536])
    for j in range(4):
        DM(wc_bc[:, j, :], wconv[j:j + 1, :].partition_broadcast(128), [], [wc_bc])
    negA = sb(p1, "negA", [128, 4])
    dtb_bc = sb(p1, "dtb_bc", [128, 4])
    DM(negA[:], alog[0:1, :].partition_broadcast(128), [], [negA])
    DM(dtb_bc[:], dtb[0:1, :].partition_broadcast(128), [], [dtb_bc])
    A(lambda e: e.activation(out=negA[:], in_=negA[:], func=AF.Exp), [negA], [negA])
    V(lambda e: e.tensor_scalar(out=negA[:], in0=negA[:], scalar1=-1.0, scalar2=None, op0=ALU.mult), [negA], [negA])
    ropes = sb(p1, "ropes", [128, NT, 16])
    DM(ropes[:], rope_seq.rearrange("(n p) c -> p n c", p=128), [], [ropes])

    xt2 = [sb(p1, "xt%d" % i, [128, D]) for i in range(2)]
    xb = sb(p1, "xb", [128, D], BF16)
    xT = sb(p1, "xT", [128, 8, 128], BF16)
    junk = ctmp
    ssq = sb(p1, "ssq", [128, 1])
    rstd = sb(p1, "rstd", [128, 1])
    tmp1 = sb(p1, "tmp1", [128, 1])
    kvi = sb(p1, "kvi", [128, 328])
    kvib = sb(p1, "kvib", [128, 320], BF16)
    ubuf = [sb(p1, "ubuf%d" % i, [128, 1536]) for i in range(2)]
    ush = [sb(p1, "ush%d" % i, [128, 1536]) for i in range(1)]
    cc = sb(p1, "cc", [128, 1536])
    rt1 = sb(p1, "rt1", [128, 3, 8])
    rt2 = sb(p1, "rt2", [128, 3, 8])
    rt3 = sb(p1, "rt3", [128, 3, 8])
    hsq = sb(p1, "hsq", [128, 3, 64])
    hss = sb(p1, "hss", [128, 3])
    hrs = sb(p1, "hrs", [128, 3])
    htm = sb(p1, "htm", [128, 3])
    ss8 = sb(p1, "ss8", [128, 8])
    rs8 = sb(p1, "rs8", [128, 8])
    tm8 = sb(p1, "tm8", [128, 8])
    gx = sb(p1, "gx", [128, 4])
    gax = sb(p1, "gax", [128, 4])
    gl_ = sb(p1, "gl_", [128, 4])
    gg = sb(p1, "gg", [128, 4])
    beta = sb(p1, "beta", [128, 4])
    nbeta = sb(p1, "nbeta", [128, 4])
    Gc = sb(p1, "Gc", [128, 4])
    eGc = sb(p1, "eGc", [128, 4])
    kdsc = sb(p1, "kdsc", [128, 4])
    Sst = sb(p1, "Sst", [128, 4, 128])
    oacc = sb(p1, "oacc", [128, 512])
    psX = ps(p1, "psX", [128, D], BF16)
    pb = [ps(p1, "pb%d" % i, [128, 512]) for i in range(7)]
    pbig = ExitStack()
    NB = 17
    big = [sb(pbig, "big%d" % i, [128, 4, 128]) for i in range(NB)]

    G(lambda e: e.memset(Sst[:], 0.0), [], [Sst])

    def dn_front(h_u_ps_list, ucur, uprev, rstd_t, nrows_prev_from_state=None):
        for i, pst in enumerate(h_u_ps_list):
            A(lambda e, i=i, pst=pst: e.activation(out=ucur[:, i * 512:(i + 1) * 512], in_=pst[:, :], func=AF.Copy,
                                                  scale=rstd_t[:, 0:1]), [pst, rstd_t], [ucur])
        V(lambda e: e.tensor_tensor(out=cc[:], in0=ucur[:], in1=wc_bc[:, 3, :], op=ALU.mult), [ucur, wc_bc], [cc])
        for d in (1, 2, 3):
            us = ush[0]
            if nrows_prev_from_state is None:
                DM(us[d:128, :], ucur[0:128 - d, :], [ucur], [us])
                if uprev is None:
                    G(lambda e, us=us, d=d: e.memset(us[0:d, :], 0.0), [], [us])
                else:
                    DM(us[0:d, :], uprev[128 - d:128, :], [uprev], [us])
            else:
                DM(us[0:NS, :], st_conv[:, 3 - d, :], [], [us])
            eng = V if d != 2 else G
            eng(lambda e, us=us, d=d: e.tensor_tensor(out=ctmp[:], in0=us[:], in1=wc_bc[:, 3 - d, :], op=ALU.mult),
                [us, wc_bc], [ctmp])
            eng(lambda e: e.tensor_tensor(out=cc[:], in0=cc[:], in1=ctmp[:], op=ALU.add), [cc, ctmp], [cc])
        A(lambda e: e.activation(out=cc[:], in_=cc[:], func=AF.Silu), [cc], [cc])
        qk3 = cc[:, 0:1024].rearrange("p (h d) -> p h d", d=128)
        A(lambda e: e.activation(out=ctmp[:, 0:1024], in_=cc[:, 0:1024], func=AF.Square), [cc], [ctmp])
        V(lambda e: e.tensor_reduce(out=ss8[:], in_=ctmp[:, 0:1024].rearrange("p (h d) -> p h d", d=128), axis=AX.X,
                                    op=ALU.add), [ctmp], [ss8])
        A(lambda e: e.activation(out=tm8[:], in_=ss8[:], func=AF.Sqrt, scale=1.0, bias=EPS), [ss8], [tm8])
        V(lambda e: e.reciprocal(out=rs8[:], in_=tm8[:]), [tm8], [rs8])
        V(lambda e: e.tensor_scalar(out=rs8[:, 0:4], in0=rs8[:, 0:4], scalar1=128.0 ** -0.5, scalar2=None, op0=ALU.mult),
          [rs8], [rs8])
        V(lambda e: e.tensor_tensor(out=qk3, in0=qk3, in1=rs8[:, :, None].to_broadcast([128, 8, 128]), op=ALU.mult),
          [cc, rs8], [cc])
        V(lambda e: e.tensor_tensor(out=gx[:], in0=kvi[:, 320:324], in1=dtb_bc[:], op=ALU.add), [kvi, dtb_bc], [gx])
        V(lambda e: e.scalar_tensor_tensor(out=gax[:], in0=gx[:], scalar=-1.0, in1=gx[:], op0=ALU.mult, op1=ALU.max),
          [gx], [gax])
        A(lambda e: e.activation(out=gax[:], in_=gax[:], func=AF.Exp, scale=-1.0), [gax], [gax])
        A(lambda e: e.activation(out=gl_[:], in_=gax[:], func=AF.Ln, bias=1.0), [gax], [gl_])
        V(lambda e: e.scalar_tensor_tensor(out=gg[:], in0=gx[:], scalar=0.0, in1=gl_[:], op0=ALU.max, op1=ALU.add),
          [gx, gl_], [gg])
        V(lambda e: e.tensor_tensor(out=gg[:], in0=gg[:], in1=negA[:], op=ALU.mult), [gg, negA], [gg])
        A(lambda e: e.activation(out=beta[:], in_=kvi[:, 324:328], func=AF.Sigmoid), [kvi], [beta])
        V(lambda e: e.tensor_scalar(out=nbeta[:], in0=beta[:], scalar1=-1.0, scalar2=None, op0=ALU.mult), [beta], [nbeta])

    def kvik_post(cs, tile_rows_out):
        k3 = kvi[:, 0:128].rearrange("p (h d) -> p h d", d=64)
        head_rms(k3, kvi, 2, kg_bc, hsq, hss, hrs, htm)
        rope((k3, kvi), 2, cs, rt1, rt2, rt3)
        i3 = kvi[:, 256:320].rearrange("p (h d) -> p h d", d=64)
        rope((i3, kvi), 1, cs, rt1, rt2, rt3)

    def bc4(t):
        return t[:, :, None].to_broadcast([128, 4, 128])

    def cb4(ap2):
        return ap2[:, None, :].to_broadcast([128, 4, 128])

    def mm4(pst, lhs, rhs, start=True, stop=True, rl=None, rr=None):
        for h in range(4):
            P(lambda e, h=h: e.matmul(pst[:, h * 128:(h + 1) * 128], lhsT=lhs[:, h, :], rhs=rhs[:, h, :],
                                     start=start, stop=stop), [lhs, rhs], [pst])

    def p3(pst):
        return pst[:, :].rearrange("p (h c) -> p h c", c=128)

    def deltanet_tile(otile_first, sel_col):
        (knT, kbT, qnT, GrowS, tB, EB, EA, EG, Bm, Am, QKT, X, Y, P2, Q2, X2, Y2) = big
        kb_ = EB
        t2 = EB
        vn = GrowS
        kdec = tB
        TbT = X2
        qn3 = cc[:, 0:512].rearrange("p (h d) -> p h d", d=128)
        kn3 = cc[:, 512:1024].rearrange("p (h d) -> p h d", d=128)
        v3 = cc[:, 1024:1536].rearrange("p (h d) -> p h d", d=128)
        P(lambda e: e.matmul(pb[4][:, 0:4], lhsT=TriC, rhs=gg[:], start=True, stop=True), [cst, gg], [pb[4]])
        V(lambda e: e.tensor_copy(out=Gc[:], in_=pb[4][:, 0:4]), [pb[4]], [Gc])
        A(lambda e: e.activation(out=eGc[:], in_=Gc[:], func=AF.Exp), [Gc], [eGc])
        V(lambda e: e.tensor_tensor(out=kb_[:], in0=kn3, in1=bc4(beta), op=ALU.mult), [cc, beta], [kb_])
        for (src, srctl, dst, pst) in ((kn3, cc, knT, pb[4]), (None, kb_, kbT, pb[5]), (qn3, cc, qnT, pb[6])):
            for h in range(4):
                inp = kb_[:, h, :] if src is None else src[:, h, :]
                P(lambda e, h=h, inp=inp, pst=pst: e.transpose(out=pst[:, h * 128:(h + 1) * 128], in_=inp, identity=ident),
                  [srctl, cst], [pst])
            A(lambda e, dst=dst, pst=pst: e.copy(out=dst[:].rearrange("p a b -> p (a b)"), in_=pst[:, :]), [pst], [dst])
        V(lambda e: e.tensor_tensor(out=tB[:], in0=cb4(ident), in1=bc4(Gc), op=ALU.mult), [cst, Gc], [tB])
        for h in range(4):
            P(lambda e, h=h: e.matmul(pb[4][:, h * 128:(h + 1) * 128], lhsT=ones, rhs=tB[:, h, :], start=True, stop=True),
              [cst, tB], [pb[4]])
        V(lambda e: e.tensor_copy(out=GrowS[:].rearrange("p a b -> p (a b)"), in_=pb[4][:, :]), [pb[4]], [GrowS])
        V(lambda e: e.tensor_tensor(out=tB[:], in0=GrowS[:], in1=bc4(Gc), op=ALU.subtract), [GrowS, Gc], [tB])
        V(lambda e: e.tensor_scalar_min(out=EB[:], in0=tB[:], scalar1=0.0), [tB], [EB])
        G(lambda e: e.tensor_scalar_max(out=EA[:], in0=tB[:], scalar1=0.0), [tB], [EA])
        A(lambda e: e.activation(out=EB[:], in_=EB[:], func=AF.Exp), [EB], [EB])
        A(lambda e: e.activation(out=EA[:], in_=EA[:], func=AF.Exp, scale=-1.0), [EA], [EA])
        A(lambda e: e.activation(out=EG[:], in_=GrowS[:], func=AF.Exp), [GrowS], [EG])
        V(lambda e: e.tensor_tensor(out=kdsc[:], in0=GrowS[:, :, 127], in1=Gc[:], op=ALU.subtract), [GrowS, Gc], [kdsc])
        A(lambda e: e.activation(out=kdsc[:], in_=kdsc[:], func=AF.Exp), [kdsc], [kdsc])
        mm4(pb[4], knT, kbT)
        mm4(pb[5], knT, qnT)
        mm4(pb[6], kbT, knT)
        G(lambda e: e.tensor_tensor(out=tB[:], in0=EB[:], in1=cb4(Ustr), op=ALU.mult), [EB, cst], [tB])
        V(lambda e: e.tensor_tensor(out=Bm[:], in0=p3(pb[4]), in1=tB[:], op=ALU.mult), [pb[4], tB], [Bm])
        G(lambda e: e.tensor_tensor(out=EB[:], in0=EB[:], in1=cb4(Uinc), op=ALU.mult), [EB, cst], [EB])
        V(lambda e: e.tensor_tensor(out=QKT[:], in0=p3(pb[5]), in1=EB[:], op=ALU.mult), [pb[5], EB], [QKT])
        G(lambda e: e.tensor_tensor(out=EA[:], in0=EA[:], in1=cb4(Lstr), op=ALU.mult), [EA, cst], [EA])
        V(lambda e: e.tensor_tensor(out=Am[:], in0=p3(pb[6]), in1=EA[:], op=ALU.mult), [pb[6], EA], [Am])
        V(lambda e: e.tensor_tensor(out=X[:], in0=cb4(ident), in1=Bm[:], op=ALU.subtract), [Bm, cst], [X])
        V(lambda e: e.tensor_tensor(out=Y[:], in0=cb4(ident), in1=Am[:], op=ALU.subtract), [Am, cst], [Y])
        V(lambda e: e.tensor_tensor(out=qnT[:], in0=qnT[:], in1=EG[:], op=ALU.mult), [qnT, EG], [qnT])
        Pm, Qm, Xc, Yc, Pn, Qn, Xn, Yn = Bm, Am, X, Y, P2, Q2, X2, Y2
        for j in range(6):
            last = (j == 5)
            mm4(pb[4], Qm, Pm)
            if not last:
                mm4(pb[5], Pm, Qm)
            A(lambda e, Pn=Pn: e.copy(out=Pn[:].rearrange("p a b -> p (a b)"), in_=pb[4][:, :]), [pb[4]], [Pn])
            if not last:
                A(lambda e, Qn=Qn: e.copy(out=Qn[:].rearrange("p a b -> p (a b)"), in_=pb[5][:, :]), [pb[5]], [Qn])
            mm4(pb[6], Yc, Pn)
            V(lambda e, Xn=Xn, Xc=Xc: e.tensor_tensor(out=Xn[:], in0=Xc[:], in1=p3(pb[6]), op=ALU.add), [Xc, pb[6]], [Xn])
            if not last:
                mm4(pb[4], Pn, Yc)
                V(lambda e, Yn=Yn, Yc=Yc: e.tensor_tensor(out=Yn[:], in0=Yc[:], in1=p3(pb[4]), op=ALU.add),
                  [Yc, pb[4]], [Yn])
            Pm, Pn = Pn, Pm
            Qm, Qn = Qn, Qm
            Xc, Xn = Xn, Xc
            Yc, Yn = Yn, Yc
        V(lambda e: e.tensor_tensor(out=TbT[:], in0=Xc[:], in1=bc4(nbeta), op=ALU.mult), [Xc, nbeta], [TbT])
        G(lambda e: e.tensor_tensor(out=kdec[:], in0=kn3, in1=bc4(kdsc), op=ALU.mult), [cc, kdsc], [kdec])
        mm4(pb[4], knT, Sst)
        V(lambda e: e.tensor_tensor(out=t2[:], in0=p3(pb[4]), in1=bc4(eGc), op=ALU.mult), [pb[4], eGc], [t2])
        V(lambda e: e.tensor_tensor(out=t2[:], in0=t2[:], in1=v3, op=ALU.subtract), [t2, cc], [t2])
        mm4(pb[5], TbT, t2)
        A(lambda e: e.copy(out=vn[:].rearrange("p a b -> p (a b)"), in_=pb[5][:, :]), [pb[5]], [vn])
        for h in range(4):
            P(lambda e, h=h: e.matmul(pb[6][:, h * 128:(h + 1) * 128], lhsT=qnT[:, h, :], rhs=Sst[:, h, :],
                                     start=True, stop=False), [qnT, Sst], [pb[6]])
            P(lambda e, h=h: e.matmul(pb[6][:, h * 128:(h + 1) * 128], lhsT=QKT[:, h, :], rhs=vn[:, h, :],
                                     start=False, stop=True), [QKT, vn], [pb[6]])
        if otile_first:
            V(lambda e: e.tensor_scalar(out=oacc[:], in0=pb[6][:, :], scalar1=sel_col, scalar2=None, op0=ALU.mult),
              [pb[6], selt], [oacc])
        else:
            V(lambda e: e.scalar_tensor_tensor(out=oacc[:], in0=pb[6][:, :], scalar=sel_col, in1=oacc[:],
                                               op0=ALU.mult, op1=ALU.add), [pb[6], selt, oacc], [oacc])
        mm4(pb[4], kdec, vn)
        V(lambda e: e.tensor_tensor(out=Sst[:], in0=Sst[:], in1=EG[:, :, 127:128].to_broadcast([128, 4, 128]),
                                    op=ALU.mult), [Sst, EG], [Sst])
        V(lambda e: e.tensor_tensor(out=Sst[:], in0=Sst[:], in1=p3(pb[4]), op=ALU.add), [Sst, pb[4]], [Sst])

    nt_run = NT if phase_limit >= 1 else 0
    nt_run = int(os.environ.get("MK_NT", nt_run))
    DM(xt2[0][:], xseq[0:128, :], [], [xt2[0]])
    for i in range(nt_run):
        xt = xt2[i % 2]
        if i + 1 < nt_run:
            DM(xt2[(i + 1) % 2][:], xseq[(i + 1) * 128:(i + 2) * 128, :], [], [xt2[(i + 1) % 2]])
        x_front(xt, xb, xT, psX, ssq, rstd, tmp1, junk)
        proj(xT, Wseq, 0, 328, pb[0])
        for c in range(3):
            proj(xT, Wseq, 328 + c * 512, 512, pb[1 + c])
        A(lambda e: e.activation(out=kvi[:], in_=pb[0][:, 0:328], func=AF.Copy, scale=rstd[:, 0:1]), [pb[0], rstd], [kvi])
        DM(vp[i * 128:(i + 1) * 128, :], kvi[:, 128:256], [kvi], [])
        V(lambda e, i=i: e.tensor_copy(out=VA[:, i, 0:64], in_=kvi[:, 128:192]), [kvi], [VA])
        V(lambda e, i=i: e.tensor_copy(out=VA[:, i, 65:129], in_=kvi[:, 192:256]), [kvi], [VA])
        kvik_post(ropes[:, i, :], None)
        DM(kp[i * 128:(i + 1) * 128, :], kvi[:, 0:128], [kvi], [])
        DM(ikp[i * 128:(i + 1) * 128, :], kvi[:, 256:320], [kvi], [])
        V(lambda e: e.tensor_copy(out=kvib[:], in_=kvi[:, 0:320]), [kvi], [kvib])
        P(lambda e: e.transpose(out=psX[:, 0:128], in_=kvib[:, 0:128], identity=identb[:]), [kvib, identb], [psX])
        P(lambda e: e.transpose(out=psX[0:64, 128:256], in_=kvib[:, 256:320], identity=identb[:]), [kvib, identb], [psX])
        A(lambda e, i=i: e.copy(out=KT[:, i * 128:(i + 1) * 128], in_=psX[:, 0:128]), [psX], [KT])
        A(lambda e, i=i: e.copy(out=IKT[:, i * 128:(i + 1) * 128], in_=psX[0:64, 128:256]), [psX], [IKT])
        ucur = ubuf[i % 2]
        uprev = ubuf[(i + 1) % 2] if i > 0 else None
        dn_front([pb[1], pb[2], pb[3]], ucur, uprev, rstd)
        deltanet_tile(i % 4 == 0, selt[:, (i % 4):(i % 4) + 1])
        if i % 4 == 3:
            DM(oown_d[i // 4], oacc[:], [oacc], [])
        if i % 8 == 7 and i + 1 < nt_run:
            S.reset()
        if i == nt_run - 1:
            DM(convp[:, :], ucur[125:128, :], [ucur], [])
            DM(deltap.rearrange("h k v -> k h v"), Sst[:], [Sst], [])

    pbig.close()
    if phase_limit >= 1 and nt_run == NT:
        S.barrier()
        psm = ExitStack()
        Sall = sb(psm, "Sall", [128, 64, 128])
        knTs = sb(psm, "knTs", [128, 4, 16])
        qnTs = sb(psm, "qnTs", [128, 4, 16])
        ksel = sb(psm, "ksel", [128, 16, 16])
        qsel = sb(psm, "qsel", [128, 16, 16])
        Rm = sb(psm, "Rm", [16, 16, 4])
        egbc = sb(psm, "egbc", [128, 64])
        egS = sb(psm, "egS", [128, 4])
        t2s = sb(psm, "t2s", [16, 128])
        vns = sb(psm, "vns", [16, 128])
        vsl = [sb(psm, "vsl%d" % i, [16, 128]) for i in range(2)]
        qks = sb(psm, "qks", [16, 1])
        tq = sb(psm, "tq", [16, 128])
        DM(Sall[:], st_delta.rearrange("b h k v -> k (b h) v"), [], [Sall])
        G(lambda e: e.memset(oacc[:], 0.0), [], [oacc])
        xt = xt2[0]
        DM(xt[:], xown[16 * 128:17 * 128, :], [], [xt])
        x_front(xt, xb, xT, psX, ssq, rstd, tmp1, junk)
        proj(xT, Wseq, 0, 328, pb[0])
        for c in range(3):
            proj(xT, Wseq, 328 + c * 512, 512, pb[1 + c])
        A(lambda e: e.activation(out=kvi[:], in_=pb[0][:, 0:328], func=AF.Copy, scale=rstd[:, 0:1]), [pb[0], rstd], [kvi])
        DM(vs_o[:, :], kvi[0:NS, 128:256], [kvi], [])
        kvik_post(ropeo[:, 16, :], None)
        DM(ks_o[:, :], kvi[0:NS, 0:128], [kvi], [])
        DM(iks_o[:, :], kvi[0:NS, 256:320], [kvi], [])
        if phase_limit >= 2:
            V(lambda e: e.tensor_copy(out=kviS[:], in_=kvi[:]), [kvi], [kviS])
        ucur = ubuf[0]
        dn_front([pb[1], pb[2], pb[3]], ucur, None, rstd, nrows_prev_from_state=True)
        DM(convs[:, 0:2, :], st_conv[:, 1:3, :], [], [])
        DM(convs[:, 2, :], ucur[0:NS, :], [ucur], [])
        A(lambda e: e.activation(out=egS[:], in_=gg[:], func=AF.Exp), [gg], [egS])
        V(lambda e: e.tensor_tensor(out=Rm[:], in0=cst[0:16, 0, 0:16, None].to_broadcast([16, 16, 4]),
                                    in1=egS[0:16, None, :].to_broadcast([16, 16, 4]), op=ALU.mult), [cst, egS], [Rm])
        P(lambda e: e.matmul(pb[4][:, 0:64], lhsT=cst[0:16, 1, :], rhs=Rm[:].rearrange("p a b -> p (a b)"),
                             start=True, stop=True), [cst, Rm], [pb[4]])
        V(lambda e: e.tensor_copy(out=egbc[:], in_=pb[4][:, 0:64]), [pb[4]], [egbc])
        for h in range(4):
            P(lambda e, h=h: e.transpose(out=pb[5][:, h * 16:(h + 1) * 16], in_=cc[0:16, 512 + h * 128:512 + (h + 1) * 128],
                                         identity=cst[0:16, 0, 0:16]), [cc, cst], [pb[5]])
            P(lambda e, h=h: e.transpose(out=pb[5][:, 64 + h * 16:64 + (h + 1) * 16], in_=cc[0:16, h * 128:(h + 1) * 128],
                                         identity=cst[0:16, 0, 0:16]), [cc, cst], [pb[5]])
        V(lambda e: e.tensor_copy(out=knTs[:].rearrange("p a b -> p (a b)"), in_=pb[5][:, 0:64]), [pb[5]], [knTs])
        V(lambda e: e.tensor_copy(out=qnTs[:].rearrange("p a b -> p (a b)"), in_=pb[5][:, 64:128]), [pb[5]], [qnTs])
        for h in range(4):
            V(lambda e, h=h: e.tensor_tensor(out=ksel[:], in0=knTs[:, h, None, :].to_broadcast([128, 16, 16]), in1=oh[:],
                                             op=ALU.mult), [knTs, oh], [ksel])
            V(lambda e, h=h: e.tensor_tensor(out=qsel[:], in0=qnTs[:, h, None, :].to_broadcast([128, 16, 16]), in1=oh[:],
                                             op=ALU.mult), [qnTs, oh], [qsel])
            for b in range(NS):
                P(lambda e, b=b, h=h: e.matmul(pb[4][0:16, 0:128], lhsT=ksel[:, b, :], rhs=Sall[:, b * 4 + h, :],
                                               start=(b == 0), stop=(b == NS - 1)), [ksel, Sall], [pb[4]])
            for b in range(NS):
                P(lambda e, b=b, h=h: e.matmul(pb[5][0:16, 0:128], lhsT=qsel[:, b, :], rhs=Sall[:, b * 4 + h, :],
                                               start=(b == 0), stop=(b == NS - 1)), [qsel, Sall], [pb[5]])
            vh = cc[0:16, 1024 + h * 128:1024 + (h + 1) * 128]
            knh = cc[0:16, 512 + h * 128:512 + (h + 1) * 128]
            qnh = cc[0:16, h * 128:(h + 1) * 128]
            V(lambda e, h=h, vh=vh: e.scalar_tensor_tensor(out=t2s[:], in0=pb[4][0:16, 0:128], scalar=egS[0:16, h:h + 1],
                                                           in1=vh, op0=ALU.mult, op1=ALU.subtract), [pb[4], egS, cc], [t2s])
            V(lambda e, h=h: e.tensor_scalar(out=vns[:], in0=t2s[:], scalar1=nbeta[0:16, h:h + 1], scalar2=None,
                                             op0=ALU.mult), [t2s, nbeta], [vns])
            V(lambda e, qnh=qnh, knh=knh: e.tensor_tensor(out=tq[:], in0=qnh, in1=knh, op=ALU.mult), [cc], [tq])
            V(lambda e: e.tensor_reduce(out=qks[:], in_=tq[:], axis=AX.X, op=ALU.add), [tq], [qks])
            V(lambda e, h=h: e.tensor_scalar(out=tq[:], in0=pb[5][0:16, 0:128], scalar1=egS[0:16, h:h + 1], scalar2=None,
                                             op0=ALU.mult), [pb[5], egS], [tq])
            V(lambda e, h=h: e.scalar_tensor_tensor(out=oacc[0:16, h * 128:(h + 1) * 128], in0=vns[:], scalar=qks[:, 0:1],
                                                    in1=tq[:], op0=ALU.mult, op1=ALU.add), [vns, qks, tq], [oacc])
            for b in range(NS):
                vs_ = vsl[b % 2]
                V(lambda e, b=b, vs_=vs_: e.tensor_scalar(out=vs_[:], in0=vns[:], scalar1=cst[0:16, 0, b:b + 1], scalar2=None,
                                                          op0=ALU.mult), [vns, cst], [vs_])
                pz = pb[(b % 2)]
                P(lambda e, vs_=vs_, knh=knh, pz=pz: e.matmul(pz[:, 0:128], lhsT=knh, rhs=vs_[:], start=True, stop=True),
                  [cc, vs_], [pz])
                V(lambda e, b=b, h=h, pz=pz: e.scalar_tensor_tensor(
                    out=Sall[:, b * 4 + h, :], in0=Sall[:, b * 4 + h, :], scalar=egbc[:, b * 4 + h:b * 4 + h + 1],
                    in1=pz[:, 0:128], op0=ALU.mult, op1=ALU.add), [Sall, egbc, pz], [Sall])
        DM(deltas.rearrange("b h k v -> k (b h) v"), Sall[:], [Sall], [])
        DM(oown_d[16], oacc[:], [oacc], [])
        S.reset()
        psm.close()

    S.reset()
    p1.close()

    def bisect(score, n, cols):
        (mn, mx, lo, wd, mid, cnt, ge) = cols
        V(lambda e: e.tensor_reduce(out=mx[:], in_=score[:, 0:n], axis=AX.X, op=ALU.max), [score], [mx])
        V(lambda e: e.tensor_scalar(out=lo[:], in0=mn[:], scalar1=-1.0, scalar2=None, op0=ALU.add), [mn], [lo])
        V(lambda e: e.tensor_tensor(out=wd[:], in0=mx[:], in1=mn[:], op=ALU.subtract), [mx, mn], [wd])
        V(lambda e: e.tensor_scalar(out=wd[:], in0=wd[:], scalar1=2.0, scalar2=None, op0=ALU.add), [wd], [wd])
        for it in range(NITER):
            V(lambda e: e.tensor_scalar(out=wd[:], in0=wd[:], scalar1=0.5, scalar2=None, op0=ALU.mult), [wd], [wd])
            V(lambda e: e.tensor_tensor(out=mid[:], in0=lo[:], in1=wd[:], op=ALU.add), [lo, wd], [mid])
            V(lambda e: e.tensor_scalar(out=maskb[:, 0:n], in0=score[:, 0:n], scalar1=mid[:, 0:1], scalar2=None,
                                        op0=ALU.is_ge, op1=ALU.add, accum_out=cnt[:]), [score, mid], [maskb, cnt])
            V(lambda e: e.tensor_scalar(out=ge[:], in0=cnt[:], scalar1=float(TOPK), scalar2=None, op0=ALU.is_ge), [cnt], [ge])
            V(lambda e: e.scalar_tensor_tensor(out=lo[:], in0=ge[:], scalar=wd[:, 0:1], in1=lo[:], op0=ALU.mult, op1=ALU.add),
              [ge, wd, lo], [lo])
        V(lambda e: e.tensor_scalar(out=maskb[:, 0:n], in0=score[:, 0:n], scalar1=lo[:, 0:1], scalar2=None, op0=ALU.is_ge),
          [score, lo], [maskb])

    def mask_transposes(nblk):
        for b0 in range(0, nblk, 8):
            nb = min(8, nblk - b0)
            for b in range(nb):
                P(lambda e, b=b, b0=b0: e.transpose(out=psX[:, b * 128:(b + 1) * 128], in_=maskb[:, (b0 + b) * 128:(b0 + b + 1) * 128],
                                                    identity=identb[:]), [maskb, identb], [psX])
            A(lambda e, b0=b0, nb=nb: e.copy(out=maskT[:, b0:b0 + nb, :].rearrange("p a b -> p (a b)"), in_=psX[:, 0:nb * 128]),
              [psX], [maskT])

    def q_side(j, qsb, QT, IQT, wsc):
        xt = xtq[j % 2]
        DM(xt[:], xown[j * 128:(j + 1) * 128, :], [], [xt])
        x_front(xt, xb, xT, psX, ssq, rstd, tmp1, junkq)
        proj(xT, Wq, 0, 512, pb[0])
        proj(xT, Wq, 512, 512, pb[1])
        proj(xT, Wq, 1024, 8, pb[2])
        A(lambda e: e.activation(out=qsb[:, 0:512], in_=pb[0][:, :], func=AF.Copy, scale=rstd[:, 0:1]), [pb[0], rstd], [qsb])
        A(lambda e: e.activation(out=qsb[:, 512:1024], in_=pb[1][:, :], func=AF.Copy, scale=rstd[:, 0:1]), [pb[1], rstd], [qsb])
        A(lambda e: e.activation(out=qsb[:, 1024:1032], in_=pb[2][:, 0:8], func=AF.Copy, scale=rstd[:, 0:1]), [pb[2], rstd], [qsb])
        q3 = qsb[:, 0:512].rearrange("p (h d) -> p h d", d=64)
        head_rms(q3, qsb, 8, qg_bc, hsq8, hss8, hrs8, htm8)
        rope((q3, qsb), 8, ropeo[:, j, :], rq1, rq2, rq3)
        i3 = qsb[:, 512:1024].rearrange("p (h d) -> p h d", d=64)
        rope((i3, qsb), 8, ropeo[:, j, :], rq1, rq2, rq3)
        V(lambda e: e.tensor_copy(out=qbf[:, 0:512].rearrange("p (g k d) -> p g k d", g=4, k=2, d=64),
                                  in_=qsb[:, 0:512].rearrange("p (k g d) -> p g k d", k=2, g=4, d=64)), [qsb], [qbf])
        V(lambda e: e.tensor_copy(out=qbf[:, 512:1024], in_=qsb[:, 512:1024]), [qsb], [qbf])
        V(lambda e: e.tensor_scalar(out=wsc[:], in0=qsb[:, 1024:1032], scalar1=8.0 ** -0.5, scalar2=None, op0=ALU.mult),
          [qsb], [wsc])
        for g in range(4):
            P(lambda e, g=g: e.transpose(out=psX[:, g * 128:(g + 1) * 128], in_=qbf[:, g * 128:(g + 1) * 128], identity=identb[:]),
              [qbf, identb], [psX])
        A(lambda e: e.copy(out=QT[:].rearrange("p a b -> p (a b)"), in_=psX[:, 0:512]), [psX], [QT])
        for h in range(8):
            P(lambda e, h=h: e.transpose(out=psX[0:64, h * 128:(h + 1) * 128], in_=qbf[:, 512 + h * 64:512 + (h + 1) * 64],
                                         identity=identb[:]), [qbf, identb], [psX])
        A(lambda e: e.copy(out=IQT[:].rearrange("p a b -> p (a b)"), in_=psX[0:64, :]), [psX], [IQT])

    if phase_limit >= 2:
        p2 = ExitStack()
        Wq = sb(p2, "Wq", [128, 8, 1032], BF16)
        stg = sb(p2, "stg", [128, 1032])
        gmix = sb(p2, "gmix2", [128, 8])
        DM(gmix[:], nmix[:, :], [], [gmix])
        load_weight(Wq, 8, w_q, 1032, stg, gain=gmix, chunk=1032)
        score = sb(p2, "score", [128, 8320])
        maskb = sb(p2, "maskb", [128, 8320], BF16)
        maskT = sb(p2, "maskT", [128, 65, 128], BF16)
        admt = sb(p2, "admt", [128, 1024])
        DM(admt[:], adm_in[:, :], [], [admt])
        xtq = [sb(p2, "xtq%d" % i, [128, D]) for i in range(2)]
        xb = sb(p2, "xb2", [128, D], BF16)
        xT = sb(p2, "xT2", [128, 8, 128], BF16)
        junkq = sb(p2, "junkq", [128, D], BF16)
        ssq = sb(p2, "ssq2", [128, 1])
        rstd = sb(p2, "rstd2", [128, 1])
        tmp1 = sb(p2, "tmp12", [128, 1])
        qsb = sb(p2, "qsb", [128, 1032])
        qbf = sb(p2, "qbf", [128, 1024], BF16)
        QT = sb(p2, "QT", [128, 4, 128], BF16)
        IQT = sb(p2, "IQT", [64, 8, 128], BF16)
        wsc = sb(p2, "wsc", [128, 8])
        hsq8 = sb(p2, "hsq8", [128, 8, 64])
        hss8 = sb(p2, "hss8", [128, 8])
        hrs8 = sb(p2, "hrs8", [128, 8])
        htm8 = sb(p2, "htm8", [128, 8])
        rq1 = sb(p2, "rq1", [128, 8, 8])
        rq2 = sb(p2, "rq2", [128, 8, 8])
        rq3 = sb(p2, "rq3", [128, 8, 8])
        rtm = [sb(p2, "rtm%d" % i, [128, 512]) for i in range(2)]
        pTt = [sb(p2, "pTt%d" % i, [128, 512], BF16) for i in range(2)]
        oTs = sb(p2, "oTs", [65, 512])
        OAT = sb(p2, "OAT", [64, 8, 128], BF16)
        bcols = [sb(p2, "bc%d" % i, [128, 1]) for i in range(7)]
        psX = ps(p2, "psX2", [128, D], BF16)
        pb = [ps(p2, "pq%d" % i, [128, 512]) for i in range(7)]
        mn = bcols[0]
        NOWN_RUN = int(os.environ.get("MK_NOWN", 16))
        for j in range(17):
            if j >= NOWN_RUN and j < 16:
                continue
            if j == 16:
                q_side(j, qsS, QTs, IQTs, wscS)
                break
            q_side(j, qsb, QT, IQT, wsc)
            ngrp = j + 1
            n = ngrp * 512
            nblk = 4 * ngrp
            k = 0
            for gi in range(ngrp):
                for h in range(8):
                    pz = pb[k % 4]
                    rt = rtm[k % 2]
                    k += 1
                    P(lambda e, h=h, gi=gi, pz=pz: e.matmul(pz[:, :], lhsT=IQT[:, h, :], rhs=IKT[:, gi * 512:(gi + 1) * 512],
                                                           start=True, stop=True), [IQT, IKT], [pz])
                    A(lambda e, pz=pz, rt=rt: e.activation(out=rt[:], in_=pz[:, :], func=AF.Relu, scale=0.125), [pz], [rt])
                    if h == 0:
                        V(lambda e, rt=rt, gi=gi: e.tensor_scalar(out=score[:, gi * 512:(gi + 1) * 512], in0=rt[:],
                                                                  scalar1=wsc[:, 0:1], scalar2=None, op0=ALU.mult),
                          [rt, wsc], [score])
                    else:
                        V(lambda e, rt=rt, gi=gi, h=h: e.scalar_tensor_tensor(
                            out=score[:, gi * 512:(gi + 1) * 512], in0=rt[:], scalar=wsc[:, h:h + 1],
                            in1=score[:, gi * 512:(gi + 1) * 512], op0=ALU.mult, op1=ALU.add), [rt, wsc, score], [score])
            last = score[:, n - 512:n]
            V(lambda e, last=last: e.tensor_tensor(out=last, in0=last, in1=admt[:, 0:512], op=ALU.mult), [score, admt], [score])
            V(lambda e, n=n: e.tensor_reduce(out=mn[:], in_=score[:, 0:n], axis=AX.X, op=ALU.min), [score], [mn])
            V(lambda e, last=last: e.tensor_tensor(out=last, in0=last, in1=admt[:, 512:1024], op=ALU.add), [score, admt], [score])
            bisect(score, n, bcols)
            mask_transposes(nblk)
            for kv in range(2):
                po = pb[4 + kv]
                for b in range(nblk):
                    pz = pb[b % 4]
                    pt_ = pTt[b % 2]
                    P(lambda e, b=b, kv=kv, pz=pz: e.matmul(pz[:, :], lhsT=KT[kv * 64:(kv + 1) * 64, b * 128:(b + 1) * 128],
                                                           rhs=QT[kv * 64:(kv + 1) * 64, :, :].rearrange("p a b -> p (a b)"),
                                                           start=True, stop=True), [KT, QT], [pz])
                    A(lambda e, pz=pz, pt_=pt_: e.activation(out=pt_[:], in_=pz[:, :], func=AF.Exp, scale=0.125), [pz], [pt_])
                    V(lambda e, b=b, pt_=pt_: e.tensor_tensor(
                        out=pt_[:].rearrange("p (g t) -> p g t", t=128), in0=pt_[:].rearrange("p (g t) -> p g t", t=128),
                        in1=maskT[:, b, None, :].to_broadcast([128, 4, 128]), op=ALU.mult), [pt_, maskT], [pt_])
                    P(lambda e, b=b, kv=kv, pt_=pt_, po=po: e.matmul(po[0:65, :], lhsT=VA[:, b, kv * 65:(kv + 1) * 65], rhs=pt_[:],
                                                                   start=(b == 0), stop=(b == nblk - 1)), [VA, pt_], [po])
                A(lambda e, po=po: e.copy(out=oTs[:], in_=po[0:65, :]), [po], [oTs])
                V(lambda e: e.reciprocal(out=oTs[64:65, :], in_=oTs[64:65, :]), [oTs], [oTs])
                P(lambda e: e.matmul(pb[6][0:64, :], lhsT=cst[64:65, 1, 0:64], rhs=oTs[64:65, :], start=True, stop=True),
                  [cst, oTs], [pb[6]])
                V(lambda e, kv=kv: e.tensor_tensor(out=OAT[:, kv * 4:(kv + 1) * 4, :].rearrange("p a b -> p (a b)"),
                                                   in0=oTs[0:64, :], in1=pb[6][0:64, :], op=ALU.mult), [oTs, pb[6]], [OAT])
            DM(oat_d[j], OAT[:].rearrange("p a b -> p (a b)"), [OAT], [])
            if j % 2 == 1:
                S.reset()
        S.reset()
        p2.close()
    S.reset()
    sA.close()


    if phase_limit >= 4:
        pS = ExitStack()
        score = sb(pS, "scoreS", [128, 8320])
        maskb = sb(pS, "maskbS", [128, 8320], BF16)
        maskT = sb(pS, "maskTS", [128, 65, 128], BF16)
        ptall = sb(pS, "ptall", [128, NS], I32)
        idxall = sb(pS, "idxall", [128, NS], I32)
        halfc = sb(pS, "halfc", [128, 1])
        wT = sb(pS, "wT", [8, 16])
        Wsel = sb(pS, "Wsel", [8, 16, 16])
        rS = [sb(pS, "rS%d" % i, [8, 512]) for i in range(2)]
        OATa = sb(pS, "OATa", [64, 8, 128], BF16)
        ksT = sb(pS, "ksT", [128, 16])
        sd1 = sb(pS, "sd1", [128, 8, 64])
        sd2 = sb(pS, "sd2", [128, 8])
        bcols = [sb(pS, "bcS%d" % i, [128, 1]) for i in range(7)]
        psX = ps(pS, "psXS", [128, D], BF16)
        pb = [ps(pS, "pS%d" % i, [128, 512]) for i in range(7)]
        DM(ptall[:], ptab2_in[:, :], [], [ptall])
        DM(halfc[:], half_in[:, :], [], [halfc])
        V(lambda e: e.tensor_scalar(out=idxall[:], in0=ptall[:], scalar1=2.0, scalar2=halfc[:, 0:1], op0=ALU.mult, op1=ALU.add),
          [ptall, halfc], [idxall])
        G(lambda e: e.memset(score[:, 0:8192], 0.0), [], [score])
        G(lambda e: e.memset(score[:, 8192:8320], -1e30), [], [score])
        G(lambda e: e.memset(OATa[:], 0.0), [], [OATa])
        P(lambda e: e.transpose(out=pb[0][0:8, 0:16], in_=wscS[0:16, 0:8], identity=cst[0:16, 0, 0:16]), [wscS, cst], [pb[0]])
        V(lambda e: e.tensor_copy(out=wT[:], in_=pb[0][0:8, 0:16]), [pb[0]], [wT])
        V(lambda e: e.tensor_tensor(out=Wsel[:], in0=wT[:, None, :].to_broadcast([8, 16, 16]), in1=oh[0:8, :, :], op=ALU.mult),
          [wT, oh], [Wsel])
        iq3 = qsS[:, 512:1024].rearrange("p (h d) -> p h d", d=64)
        V(lambda e: e.tensor_tensor(out=sd1[:], in0=iq3, in1=kviS[:, None, 256:320].to_broadcast([128, 8, 64]), op=ALU.mult),
          [qsS, kviS], [sd1])
        V(lambda e: e.tensor_reduce(out=sd2[:], in_=sd1[:], axis=AX.X, op=ALU.add), [sd1], [sd2])
        V(lambda e: e.tensor_scalar(out=sd2[:], in0=sd2[:], scalar1=0.125, scalar2=0.0, op0=ALU.mult, op1=ALU.max), [sd2], [sd2])
        V(lambda e: e.tensor_tensor(out=sd2[:], in0=sd2[:], in1=wscS[:], op=ALU.mult), [sd2, wscS], [sd2])
        V(lambda e: e.tensor_reduce(out=score[:, 8192:8193], in_=sd2[:], axis=AX.X, op=ALU.add), [sd2], [score])
        pA = ExitStack()
        IKg = sb(pA, "IKg", [128, 64, 64])
        IKTb = sb(pA, "IKTb", [64, 64, 128], BF16)
        for b in range(NS):
            S.dma(lambda e, b=b: e.indirect_dma_start(
                out=IKg[:].rearrange("p a b -> p (a b)"), out_offset=None, in_=cache_ik[:, :],
                in_offset=bass.IndirectOffsetOnAxis(ap=idxall[:, b:b + 1], axis=0)), [idxall], [IKg], en='pool')
            for j0 in range(0, 64, 4):
                pz = pb[(j0 // 4) % 2]
                for jj in range(4):
                    P(lambda e, j0=j0, jj=jj, pz=pz: e.transpose(out=pz[0:64, jj * 128:(jj + 1) * 128], in_=IKg[:, j0 + jj, :],
                                                                 identity=ident), [IKg, cst], [pz])
                A(lambda e, j0=j0, pz=pz: e.copy(out=IKTb[:, j0:j0 + 4, :].rearrange("p a b -> p (a b)"), in_=pz[0:64, :]),
                  [pz], [IKTb])
            for jg in range(16):
                p1_ = pb[2 + jg % 2]
                p2_ = pb[4 + jg % 2]
                r_ = rS[jg % 2]
                P(lambda e, b=b, jg=jg, p1_=p1_: e.matmul(p1_[0:8, :], lhsT=IQTs[:, :, b],
                                                         rhs=IKTb[:, jg * 4:(jg + 1) * 4, :].rearrange("p a b -> p (a b)"),
                                                         start=True, stop=True), [IQTs, IKTb], [p1_])
                A(lambda e, p1_=p1_, r_=r_: e.activation(out=r_[:], in_=p1_[0:8, :], func=AF.Relu, scale=0.125), [p1_], [r_])
                P(lambda e, b=b, r_=r_, p2_=p2_: e.matmul(p2_[0:16, :], lhsT=Wsel[:, b, :], rhs=r_[:], start=True, stop=True),
                  [Wsel, r_], [p2_])
                V(lambda e, jg=jg, p2_=p2_: e.tensor_tensor(out=score[0:16, jg * 512:(jg + 1) * 512],
                                                            in0=score[0:16, jg * 512:(jg + 1) * 512], in1=p2_[0:16, :],
                                                            op=ALU.add), [score, p2_], [score])
            if b % 4 == 3:
                S.reset()
        S.reset()
        pA.close()
        mn = bcols[0]
        V(lambda e: e.tensor_reduce(out=mn[:], in_=score[:, 0:8193], axis=AX.X, op=ALU.min), [score], [mn])
        bisect(score, 8320, bcols)
        mask_transposes(65)
        S.reset()
        pB = ExitStack()
        Kg = sb(pB, "Kg", [128, 64, 128])
        Vg = sb(pB, "Vg", [128, 64, 128])
        KTb = sb(pB, "KTb", [128, 65, 128], BF16)
        Vself = sb(pB, "Vself", [128, NS, 128])
        pSx = sb(pB, "pSx", [128, 65, 4])
        pssum = sb(pB, "pssum", [128, 4])
        oTq = sb(pB, "oTq", [64, 4])
        rdq = sb(pB, "rdq", [64, 4])
        G(lambda e: e.memset(Vself[:], 0.0), [], [Vself])
        G(lambda e: e.memset(KTb[:, 64, :], 0.0), [], [KTb])
        DM(Vself[0:1, :, :], kviS[0:NS, 128:256], [kviS], [Vself])
        P(lambda e: e.transpose(out=pb[0][:, 0:16], in_=kviS[0:16, 0:128], identity=cst[0:16, 0, 0:16]), [kviS, cst], [pb[0]])
        V(lambda e: e.tensor_copy(out=ksT[:], in_=pb[0][:, 0:16]), [pb[0]], [ksT])
        for b in range(NS):
            S.dma(lambda e, b=b: e.indirect_dma_start(
                out=Kg[:].rearrange("p a b -> p (a b)"), out_offset=None, in_=cache_k[:, :],
                in_offset=bass.IndirectOffsetOnAxis(ap=idxall[:, b:b + 1], axis=0)), [idxall], [Kg], en='pool')
            S.dma(lambda e, b=b: e.indirect_dma_start(
                out=Vg[:].rearrange("p a b -> p (a b)"), out_offset=None, in_=cache_v[:, :],
                in_offset=bass.IndirectOffsetOnAxis(ap=idxall[:, b:b + 1], axis=0)), [idxall], [Vg], en='pool')
            for j0 in range(0, 64, 4):
                pz = pb[(j0 // 4) % 2]
                for jj in range(4):
                    P(lambda e, j0=j0, jj=jj, pz=pz: e.transpose(out=pz[:, jj * 128:(jj + 1) * 128], in_=Kg[:, j0 + jj, :],
                                                                 identity=ident), [Kg, cst], [pz])
                A(lambda e, j0=j0, pz=pz: e.copy(out=KTb[:, j0:j0 + 4, :].rearrange("p a b -> p (a b)"), in_=pz[:, :]),
                  [pz], [KTb])
            V(lambda e, b=b: e.tensor_copy(out=KTb[:, 64, 0:1], in_=ksT[:, b:b + 1]), [ksT], [KTb])
            for kv in range(2):
                pl = pb[2 + kv]
                for j in range(65):
                    P(lambda e, j=j, kv=kv, b=b, pl=pl: e.matmul(pl[:, j * 4:(j + 1) * 4], lhsT=KTb[kv * 64:(kv + 1) * 64, j, :],
                                                                rhs=QTs[kv * 64:(kv + 1) * 64, :, b], start=True, stop=True),
                      [KTb, QTs], [pl])
                A(lambda e, pl=pl: e.activation(out=pSx[:].rearrange("p a b -> p (a b)"), in_=pl[:, 0:260], func=AF.Exp,
                                                scale=0.125), [pl], [pSx])
                V(lambda e, b=b: e.tensor_tensor(out=pSx[:], in0=pSx[:], in1=maskT[:, :, b:b + 1].to_broadcast([128, 65, 4]),
                                                 op=ALU.mult), [pSx, maskT], [pSx])
                V(lambda e: e.tensor_reduce(out=pssum[:], in_=pSx[:].rearrange("p j g -> p g j"), axis=AX.X, op=ALU.add),
                  [pSx], [pssum])
                po = pb[4 + kv]
                for j in range(65):
                    lh = Vg[:, j, kv * 64:(kv + 1) * 64] if j < 64 else Vself[:, b, kv * 64:(kv + 1) * 64]
                    P(lambda e, j=j, lh=lh, po=po: e.matmul(po[0:64, 0:4], lhsT=lh, rhs=pSx[:, j, :], start=(j == 0),
                                                           stop=(j == 64)), [Vg, Vself, pSx], [po])
                P(lambda e: e.matmul(pb[6][0:64, 0:4], lhsT=cst[:, 1, 0:64], rhs=pssum[:], start=True, stop=True),
                  [cst, pssum], [pb[6]])
                V(lambda e: e.reciprocal(out=rdq[:], in_=pb[6][0:64, 0:4]), [pb[6]], [rdq])
                V(lambda e, po=po, kv=kv, b=b: e.tensor_tensor(out=OATa[:, kv * 4:(kv + 1) * 4, b], in0=po[0:64, 0:4], in1=rdq[:],
                                                               op=ALU.mult), [po, rdq], [OATa])
            if b % 4 == 3:
                S.reset()
        DM(oat_d[16], OATa[:].rearrange("p a b -> p (a b)"), [OATa], [])
        S.reset()
        pB.close()
        pS.close()

    if phase_limit >= 3:
        p3s = ExitStack()
        Wgz = sb(p3s, "Wgz", [128, 8, 2560], BF16)
        Wb0 = sb(p3s, "Wb0", [64, 8, 1024], BF16)
        Wb1 = sb(p3s, "Wb1", [128, 4, 1024], BF16)
        Wout = sb(p3s, "Wout", [128, 8, 1024], BF16)
        stg = sb(p3s, "stg3", [128, 2048])
        gmix = sb(p3s, "gmix3", [128, 8])
        DM(gmix[:], nmix[:, :], [], [gmix])
        load_weight(Wgz, 8, w_gz, 2560, stg, gain=gmix, chunk=1280)
        load_weight(Wout, 8, w_out, 1024, stg, chunk=1024)
        load_weight(Wb1, 4, w_br[1], 1024, stg, chunk=1024)
        for h in range(8):
            DM(stg[0:64, 0:1024], w_br[0, h * 64:(h + 1) * 64, :], [], [stg])
            V(lambda e, h=h: e.tensor_copy(out=Wb0[:, h, :], in_=stg[0:64, 0:1024]), [stg], [Wb0])
        dng_bc = sb(p3s, "dng_bc", [128, 128])
        DM(dng_bc[:], dn_g[0:1, :].partition_broadcast(128), [], [dng_bc])
        xt3 = [sb(p3s, "xt3%d" % i, [128, D]) for i in range(2)]
        xb = sb(p3s, "xb3", [128, D], BF16)
        xT = sb(p3s, "xT3", [128, 8, 128], BF16)
        junk3 = sb(p3s, "junk3", [128, D], BF16)
        ssq = sb(p3s, "ssq3", [128, 1])
        rstd = sb(p3s, "rstd3", [128, 1])
        tmp1 = sb(p3s, "tmp13", [128, 1])
        gsb = sb(p3s, "gsb", [128, 2048])
        zs = sb(p3s, "zs", [128, 512])
        ob = sb(p3s, "ob", [128, 512])
        osq = sb(p3s, "osq", [128, 512])
        ss4 = sb(p3s, "ss4", [128, 4])
        rs4 = sb(p3s, "rs4", [128, 4])
        tm4 = sb(p3s, "tm4", [128, 4])
        obb = sb(p3s, "obb", [128, 512], BF16)
        OBT = sb(p3s, "OBT", [128, 4, 128], BF16)
        OATs = sb(p3s, "OATs", [64, 8, 128], BF16)
        m1 = sb(p3s, "m1", [128, 512])
        m2 = sb(p3s, "m2", [128, 512])
        mb = sb(p3s, "mb", [128, D], BF16)
        mT = sb(p3s, "mT", [128, 8, 128], BF16)
        x1t = sb(p3s, "x1t", [128, D])
        psX = ps(p3s, "psX3", [128, D], BF16)
        pb = [ps(p3s, "pm%d" % i, [128, 512]) for i in range(7)]
        DM(xt3[0][:], xown[0:128, :], [], [xt3[0]])
        for j in range(17):
            xt = xt3[j % 2]
            if j + 1 < 17:
                DM(xt3[(j + 1) % 2][:], xown[(j + 1) * 128:(j + 2) * 128, :], [], [xt3[(j + 1) % 2]])
            DM(ob[:], oown_d[j], [], [ob])
            DM(OATs[:].rearrange("p a b -> p (a b)"), oat_d[j], [], [OATs])
            x_front(xt, xb, xT, psX, ssq, rstd, tmp1, junk3)
            for c in range(5):
                proj(xT, Wgz, c * 512, 512, pb[c])
            for c in range(4):
                A(lambda e, c=c: e.activation(out=gsb[:, c * 512:(c + 1) * 512], in_=pb[c][:, :], func=AF.Sigmoid,
                                              scale=rstd[:, 0:1]), [pb[c], rstd], [gsb])
            A(lambda e: e.activation(out=zs[:], in_=pb[4][:, :], func=AF.Silu, scale=rstd[:, 0:1]), [pb[4], rstd], [zs])
            V(lambda e: e.tensor_tensor(out=osq[:], in0=ob[:], in1=ob[:], op=ALU.mult), [ob], [osq])
            V(lambda e: e.tensor_reduce(out=ss4[:], in_=osq[:].rearrange("p (h d) -> p h d", d=128), axis=AX.X, op=ALU.add),
              [osq], [ss4])
            A(lambda e: e.activation(out=tm4[:], in_=ss4[:], func=AF.Sqrt, scale=1.0 / 128, bias=EPS), [ss4], [tm4])
            V(lambda e: e.reciprocal(out=rs4[:], in_=tm4[:]), [tm4], [rs4])
            o3 = ob[:].rearrange("p (h d) -> p h d", d=128)
            V(lambda e: e.tensor_tensor(out=o3, in0=o3, in1=rs4[:, :, None].to_broadcast([128, 4, 128]), op=ALU.mult),
              [ob, rs4], [ob])
            V(lambda e: e.tensor_tensor(out=o3, in0=o3, in1=dng_bc[:, None, :].to_broadcast([128, 4, 128]), op=ALU.mult),
              [ob, dng_bc], [ob])
            V(lambda e: e.tensor_tensor(out=obb[:], in0=ob[:], in1=zs[:], op=ALU.mult), [ob, zs], [obb])
            for h in range(4):
                P(lambda e, h=h: e.transpose(out=psX[:, h * 128:(h + 1) * 128], in_=obb[:, h * 128:(h + 1) * 128],
                                             identity=identb[:]), [obb, identb], [psX])
            A(lambda e: e.copy(out=OBT[:].rearrange("p a b -> p (a b)"), in_=psX[:, 0:512]), [psX], [OBT])
            for half in range(2):
                hs = slice(half * 512, (half + 1) * 512)
                for h in range(8):
                    P(lambda e, h=h, hs=hs: e.matmul(pb[5][:, :], lhsT=OATs[:, h, :], rhs=Wb0[:, h, hs], start=(h == 0),
                                                     stop=(h == 7)), [OATs, Wb0], [pb[5]])
                for h in range(4):
                    P(lambda e, h=h, hs=hs: e.matmul(pb[6][:, :], lhsT=OBT[:, h, :], rhs=Wb1[:, h, hs], start=(h == 0),
                                                     stop=(h == 3)), [OBT, Wb1], [pb[6]])
                V(lambda e, hs=hs: e.tensor_tensor(out=m1[:], in0=pb[5][:, :], in1=gsb[:, hs], op=ALU.mult), [pb[5], gsb], [m1])
                V(lambda e, half=half: e.tensor_tensor(out=m2[:], in0=pb[6][:, :], in1=gsb[:, 1024 + half * 512:1024 + (half + 1) * 512],
                                                       op=ALU.mult), [pb[6], gsb], [m2])
                V(lambda e, hs=hs: e.tensor_tensor(out=mb[:, hs], in0=m1[:], in1=m2[:], op=ALU.add), [m1, m2], [mb])
            for kc in range(8):
                P(lambda e, kc=kc: e.transpose(out=psX[:, kc * 128:(kc + 1) * 128], in_=mb[:, kc * 128:(kc + 1) * 128],
                                               identity=identb[:]), [mb, identb], [psX])
            A(lambda e: e.copy(out=mT[:].rearrange("p a b -> p (a b)"), in_=psX[:]), [psX], [mT])
            for half in range(2):
                hs = slice(half * 512, (half + 1) * 512)
                for kc in range(8):
                    P(lambda e, kc=kc, hs=hs, half=half: e.matmul(pb[half][:, :], lhsT=mT[:, kc, :], rhs=Wout[:, kc, hs],
                                                                 start=(kc == 0), stop=(kc == 7)), [mT, Wout], [pb[half]])
                V(lambda e, hs=hs, half=half: e.tensor_tensor(out=x1t[:, hs], in0=xt[:, hs], in1=pb[half][:, :], op=ALU.add),
                  [xt, pb[half]], [x1t])
            DM(x1_d[j * 128:(j + 1) * 128, :], x1t[:], [x1t], [])
        S.reset()
        p3s.close()

        p4s = ExitStack()
        Wgu = sb(p4s, "Wgu", [128, 8, 2 * DFF], BF16)
        Wd = sb(p4s, "Wd", [128, 22, 1024], BF16)
        stg = sb(p4s, "stg4", [128, 2048])
        gffn = sb(p4s, "gffn", [128, 8])
        DM(gffn[:], nffn[:, :], [], [gffn])
        load_weight(Wgu, 8, w_gu, 2 * DFF, stg, gain=gffn, chunk=2048)
        load_weight(Wd, 22, w_dn, 1024, stg, chunk=1024)
        xt4 = [sb(p4s, "xt4%d" % i, [128, D]) for i in range(2)]
        xb = sb(p4s, "xb4", [128, D], BF16)
        xT = sb(p4s, "xT4", [128, 8, 128], BF16)
        junk4 = sb(p4s, "junk4", [128, D], BF16)
        ssq = sb(p4s, "ssq4", [128, 1])
        rstd = sb(p4s, "rstd4", [128, 1])
        tmp1 = sb(p4s, "tmp14", [128, 1])
        sgt = [sb(p4s, "sgt%d" % i, [128, 512]) for i in range(2)]
        hmb = sb(p4s, "hmb", [128, DFF], BF16)
        hmT = sb(p4s, "hmT", [128, 22, 128], BF16)
        yt = sb(p4s, "yt", [128, D])
        psX = ps(p4s, "psX4", [128, D], BF16)
        pb = [ps(p4s, "pf%d" % i, [128, 512]) for i in range(7)]
        DM(xt4[0][:], x1_d[0:128, :], [], [xt4[0]])
        for j in range(17):
            xt = xt4[j % 2]
            if j + 1 < 17:
                DM(xt4[(j + 1) % 2][:], x1_d[(j + 1) * 128:(j + 2) * 128, :], [], [xt4[(j + 1) % 2]])
            x_front(xt, xb, xT, psX, ssq, rstd, tmp1, junk4)
            for c in range(6):
                cw = 512 if c < 5 else 256
                pa = pb[(2 * c) % 6]
                pu = pb[(2 * c + 1) % 6]
                sg = sgt[c % 2]
                proj(xT, Wgu, c * 512, cw, pa)
                proj(xT, Wgu, DFF + c * 512, cw, pu)
                A(lambda e, pa=pa, sg=sg, cw=cw: e.activation(out=sg[:, 0:cw], in_=pa[:, 0:cw], func=AF.Silu, scale=rstd[:, 0:1]),
                  [pa, rstd], [sg])
                V(lambda e, pu=pu, sg=sg, cw=cw, c=c: e.scalar_tensor_tensor(
                    out=hmb[:, c * 512:c * 512 + cw], in0=pu[:, 0:cw], scalar=rstd[:, 0:1], in1=sg[:, 0:cw],
                    op0=ALU.mult, op1=ALU.mult), [pu, rstd, sg], [hmb])
            for c0 in range(0, 22, 8):
                nb = min(8, 22 - c0)
                for c in range(nb):
                    P(lambda e, c=c, c0=c0: e.transpose(out=psX[:, c * 128:(c + 1) * 128],
                                                        in_=hmb[:, (c0 + c) * 128:(c0 + c + 1) * 128], identity=identb[:]),
                      [hmb, identb], [psX])
                A(lambda e, c0=c0, nb=nb: e.copy(out=hmT[:, c0:c0 + nb, :].rearrange("p a b -> p (a b)"), in_=psX[:, 0:nb * 128]),
                  [psX], [hmT])
            for half in range(2):
                hs = slice(half * 512, (half + 1) * 512)
                for c in range(22):
                    P(lambda e, c=c, hs=hs, half=half: e.matmul(pb[6][:, :], lhsT=hmT[:, c, :], rhs=Wd[:, c, hs],
                                                               start=(c == 0), stop=(c == 21)), [hmT, Wd], [pb[6]])
                V(lambda e, hs=hs: e.tensor_tensor(out=yt[:, hs], in0=xt[:, hs], in1=pb[6][:, :], op=ALU.add), [xt, pb[6]], [yt])
            DM(y_own[j * 128:(j + 1) * 128, :], yt[:], [yt], [])
        S.reset()
        p4s.close()

    S.barrier()
    es.close()
    return nc


def _consts():
    p = np.arange(128)
    ident = np.eye(128, dtype=np.float32)
    ones = np.ones((128, 128), np.float32)
    ustr = (p[None, :] > p[:, None]).astype(np.float32)
    uinc = (p[None, :] >= p[:, None]).astype(np.float32)
    lstr = (p[None, :] < p[:, None]).astype(np.float32)
    tric = (p[:, None] <= p[None, :]).astype(np.float32)
    cst = np.stack([ident, ones, ustr, uinc, lstr, tric], axis=1).reshape(128, 6 * 128)
    oh = np.zeros((128, 16, 16), np.float32)
    for b in range(16):
        oh[:, b, b] = 1.0
    return np.ascontiguousarray(cst), oh.reshape(128, 256), p.astype(np.float32).reshape(128, 1)


def _rope_table(pos):
    half = 8
    inv_freq = (np.float32(500000.0) ** (-np.arange(half, dtype=np.float32) / np.float32(half))).astype(np.float32)
    ang = pos.astype(np.float32)[:, None] * inv_freq[None, :]
    return np.concatenate([np.cos(ang), np.sin(ang)], axis=1).astype(np.float32)


_NC_CACHE = {}


def kernel(x_prompt, x_sample, cache_k, cache_v, cache_idx_k, state_conv, state_delta, page_table,
           norm_mix, w_in, q_norm, k_norm, w_conv, a_log, dt_bias, delta_norm, w_branch, w_out,
           norm_ffn, w_gate_up, w_down):
    f = lambda a: np.ascontiguousarray(np.asarray(a))
    x_prompt, x_sample = f(x_prompt), f(x_sample)
    w_in0 = f(w_in)[0]
    w_seq = np.ascontiguousarray(np.concatenate([w_in0[:, 512:768], w_in0[:, 1280:1344], w_in0[:, 2888:2896],
                                                 w_in0[:, 1352:2888]], axis=1))
    w_q = np.ascontiguousarray(np.concatenate([w_in0[:, 0:512], w_in0[:, 768:1280], w_in0[:, 1344:1352]], axis=1))
    w_gz = np.ascontiguousarray(np.concatenate([w_in0[:, 3408:5456], w_in0[:, 2896:3408]], axis=1))
    cst, oh, iota = _consts()
    rope_all = _rope_table(np.arange(T + 1))
    ck = f(cache_k)[0].reshape(NPOOL * 2, 64 * 128)
    cv = f(cache_v)[0].reshape(NPOOL * 2, 64 * 128)
    cik = f(cache_idx_k)[0].reshape(NPOOL * 2, 64 * 64)
    pt = f(page_table).astype(np.int32)
    in_maps = []
    p = np.arange(128)
    for c in range(8):
        s, r = c // 4, c % 4
        own_tiles = [4 * j + r for j in range(NOWN)]
        xo = np.zeros((17 * 128, D), np.float32)
        ro = np.zeros((17 * 128, 16), np.float32)
        for j, ti in enumerate(own_tiles):
            xo[j * 128:(j + 1) * 128] = x_prompt[s, ti * 128:(ti + 1) * 128]
            ro[j * 128:(j + 1) * 128] = rope_all[ti * 128:(ti + 1) * 128]
        xo[16 * 128:16 * 128 + NS] = x_sample[c * NS:(c + 1) * NS, 0]
        ro[16 * 128:17 * 128] = rope_all[T]
        sel = np.zeros((128, 4), np.float32)
        sel[:, r] = 1.0
        kk = np.arange(512)[None, :]
        adm01 = (kk <= (r * 128 + p)[:, None]).astype(np.float32)
        negbig = (adm01 - 1.0) * np.float32(1e30)
        adm = np.ascontiguousarray(np.concatenate([adm01, negbig], axis=1).astype(np.float32))
        in_maps.append(dict(
            xseq=x_prompt[s], xown=xo, rope_seq=rope_all[:T], rope_own=ro, sel=sel, adm=adm, cst=cst, oh=oh, iota=iota,
            w_seq=w_seq, w_q=w_q, w_gz=w_gz,
            nmix=np.ascontiguousarray(f(norm_mix)[0].reshape(8, 128).T),
            nffn=np.ascontiguousarray(f(norm_ffn)[0].reshape(8, 128).T),
            qn_g=f(q_norm).reshape(1, 64), kn_g=f(k_norm).reshape(1, 64), wconv=f(w_conv)[0],
            alog=f(a_log).reshape(1, 4), dtb=f(dt_bias).reshape(1, 4), dn_g=f(delta_norm).reshape(1, 128),
            w_br=f(w_branch)[0], w_out=f(w_out)[0], w_gu=f(w_gate_up)[0], w_dn=f(w_down)[0],
            cache_k=ck, cache_v=cv, cache_ik=cik,
            ptab2=np.ascontiguousarray(np.repeat(pt[c * NS:(c + 1) * NS], 2, axis=1).T.astype(np.int32)),
            halfc=(p % 2).astype(np.float32).reshape(128, 1),
            st_conv=f(state_conv)[0, c * NS:(c + 1) * NS], st_delta=f(state_delta)[0, c * NS:(c + 1) * NS]))
    PL = int(os.environ.get("MK_PHASE", 99))
    if PL < 4:
        for m in in_maps:
            for kk_ in ("cache_k", "cache_v", "cache_ik"):
                m.pop(kk_)
    if 'nc' not in _NC_CACHE:
        _NC_CACHE['nc'] = build(PL)
    nc = _NC_CACHE['nc']
    res = run_bass_kernel_spmd(nc, in_maps, core_ids=list(range(8))).results
    y_prompt = np.zeros((2, T, D), np.float32)
    y_sample = np.zeros((128, 1, D), np.float32)
    k_p = np.zeros((1, 2, T, 2, 64), np.float32)
    v_p = np.zeros((1, 2, T, 2, 64), np.float32)
    ik_p = np.zeros((1, 2, T, 64), np.float32)
    conv_p = np.zeros((1, 2, 3, 1536), np.float32)
    delta_p = np.zeros((1, 2, 4, 128, 128), np.float32)
    k_s = np.zeros((1, 128, 1, 2, 64), np.float32)
    v_s = np.zeros((1, 128, 1, 2, 64), np.float32)
    ik_s = np.zeros((1, 128, 1, 64), np.float32)
    conv_s = np.zeros((1, 128, 3, 1536), np.float32)
    delta_s = np.zeros((1, 128, 4, 128, 128), np.float32)
    for c in range(8):
        s, r = c // 4, c % 4
        R = res[c]
        yo = R["y_own"]
        for j in range(NOWN):
            ti = 4 * j + r
            y_prompt[s, ti * 128:(ti + 1) * 128] = yo[j * 128:(j + 1) * 128]
        y_sample[c * NS:(c + 1) * NS, 0] = yo[16 * 128:16 * 128 + NS]
        if r == 0:
            k_p[0, s] = R["kp"].reshape(T, 2, 64)
            v_p[0, s] = R["vp"].reshape(T, 2, 64)
            ik_p[0, s] = R["ikp"]
            conv_p[0, s] = R["convp"]
            delta_p[0, s] = R["deltap"]
        k_s[0, c * NS:(c + 1) * NS, 0] = R["ks"].reshape(NS, 2, 64)
        v_s[0, c * NS:(c + 1) * NS, 0] = R["vs"].reshape(NS, 2, 64)
        ik_s[0, c * NS:(c + 1) * NS, 0] = R["iks"]
        conv_s[0, c * NS:(c + 1) * NS] = R["convs"]
        delta_s[0, c * NS:(c + 1) * NS] = R["deltas"]
    return (y_prompt, y_sample, k_p, v_p, ik_p, conv_p, delta_p, k_s, v_s, ik_s, conv_s, delta_s)
```
